# Optimizing a Trainium2 kernel written in Bass

```python
import jax, jax.numpy as jnp
from jax import lax
import numpy as np

D_MODEL = 1024
BATCH = 8
SEQ = 8192
DEPTH = 2

GRID_W = 64
CTX_LEN = 256
HEAD_DIM = 64
M_WIDTH = D_MODEL // 4
M_HEADS = M_WIDTH // HEAD_DIM
M_CHUNK = 64
N_DIR = 2
A_WIDTH = D_MODEL // 2
A_HEADS = A_WIDTH // HEAD_DIM
A_KV_HEADS = A_HEADS // 4
A_GROUP = A_HEADS // A_KV_HEADS
WINDOW = 128
Q_BLOCK = 128
ROPE_THETA = 10000.0
C_WIDTH = D_MODEL // 4
CONV_K = 31
MIX_WIDTH = M_WIDTH + A_WIDTH + C_WIDTH
FFN_HIDDEN = ((8 * D_MODEL + 3 * 256 - 1) // (3 * 256)) * 256
IN_SPLITS = (M_WIDTH, M_WIDTH, M_WIDTH, M_WIDTH, N_DIR * 2 * M_HEADS,
             A_WIDTH, A_KV_HEADS * HEAD_DIM, A_KV_HEADS * HEAD_DIM, 2 * C_WIDTH)
IN_COLS = sum(IN_SPLITS)
EPS = 1e-6
NEG_INF = -1e30

kernel_name = 'hybrid_mlstm_swa_conformer_dit'


def rmsnorm(x, g):
    xf = x.astype(jnp.float32)
    y = xf * lax.rsqrt(jnp.mean(xf * xf, axis=-1, keepdims=True) + EPS)
    return (y * g.astype(jnp.float32)).astype(x.dtype)


def modulate(h, shift, scale):
    return h * (1 + scale) + shift


def split_cols(p):
    idx = np.cumsum(IN_SPLITS)[:-1].tolist()
    return jnp.split(p, idx, axis=-1)


def heads(a, n):
    return a.reshape(a.shape[:2] + (n, HEAD_DIM))


def rope_axis(x, pos):
    half = x.shape[-1] // 2
    freqs = ROPE_THETA ** (-jnp.arange(half, dtype=jnp.float32) / half)
    ang = pos.astype(jnp.float32)[:, None] * freqs[None, :]
    cos = jnp.cos(ang)[None, :, None, :]
    sin = jnp.sin(ang)[None, :, None, :]
    xf = x.astype(jnp.float32)
    x1, x2 = xf[..., :half], xf[..., half:]
    return jnp.concatenate([x1 * cos - x2 * sin, x2 * cos + x1 * sin], axis=-1).astype(x.dtype)


def rope_2d(x, rows, cols):
    d = x.shape[-1] // 2
    return jnp.concatenate([rope_axis(x[..., :d], rows), rope_axis(x[..., d:], cols)], axis=-1)


def mlstm_scan(q, k, v, li, lf, state):
    b_, nh, L, dh = q.shape
    nc = L // M_CHUNK

    def chunks(a):
        return jnp.moveaxis(a.reshape(a.shape[:2] + (nc, M_CHUNK) + a.shape[3:]), 2, 0)

    tril = jnp.tril(jnp.ones((M_CHUNK, M_CHUNK), dtype=bool))

    def step(carry, inp):
        C, n, m = carry
        qc, kc, vc, lic, lfc = inp
        bcum = jnp.cumsum(lfc, axis=-1)
        dmat = jnp.where(tril, bcum[..., :, None] - bcum[..., None, :] + lic[..., None, :], -jnp.inf)
        inter = bcum + m[..., None]
        m_t = jnp.maximum(inter, jnp.max(dmat, axis=-1))
        s = jnp.einsum('bhtd,bhsd->bhts', qc, kc) * jnp.exp(dmat - m_t[..., None])
        w = jnp.exp(inter - m_t)
        num = w[..., None] * jnp.einsum('bhed,bhtd->bhte', C, qc) + jnp.einsum('bhts,bhse->bhte', s, vc)
        den = w * jnp.einsum('bhd,bhtd->bht', n, qc) + jnp.sum(s, axis=-1)
        h = num / jnp.maximum(jnp.abs(den), jnp.exp(-m_t))[..., None]
        b_end = bcum[..., -1]
        g = b_end[..., None] - bcum + lic
        m_new = jnp.maximum(b_end + m, jnp.max(g, axis=-1))
        decay = jnp.exp(b_end + m - m_new)
        wg = jnp.exp(g - m_new[..., None])
        C_new = decay[..., None, None] * C + jnp.einsum('bhs,bhse,bhsd->bhed', wg, vc, kc)
        n_new = decay[..., None] * n + jnp.einsum('bhs,bhsd->bhd', wg, kc)
        return (C_new, n_new, m_new), h

    state, h = lax.scan(step, state, tuple(chunks(a) for a in (q, k, v, li, lf)))
    h = jnp.moveaxis(h, 0, 2).reshape(b_, nh, L, dh)
    return h, state


def mlstm_prep(q, k, v, g, gate_bias):
    b_, L = q.shape[0], q.shape[1]

    def to_heads(a):
        return jnp.swapaxes(a.astype(jnp.float32).reshape(b_, L, M_HEADS, HEAD_DIM), 1, 2)

    gg = g.astype(jnp.float32).reshape(b_, L, N_DIR, 2, M_HEADS) + gate_bias.astype(jnp.float32)
    gg = jnp.transpose(gg, (2, 3, 0, 4, 1))
    li = gg[:, 0]
    lf = jax.nn.log_sigmoid(gg[:, 1])
    return to_heads(q), to_heads(k) * HEAD_DIM ** -0.5, to_heads(v), li, lf


def mlstm_bidir(xs, cs):
    qx, kx, vx, lix, lfx = xs
    qc, kc, vc, lic, lfc = cs
    b_ = qx.shape[0]
    zero = (jnp.zeros((b_, M_HEADS, HEAD_DIM, HEAD_DIM), jnp.float32),
            jnp.zeros((b_, M_HEADS, HEAD_DIM), jnp.float32),
            jnp.zeros((b_, M_HEADS), jnp.float32))
    hx, hc = None, None
    for d in range(N_DIR):
        fl = (lambda a: jnp.flip(a, axis=2)) if d == 1 else (lambda a: a)
        h_c, st = mlstm_scan(fl(qc), fl(kc), fl(vc), fl(lic[d]), fl(lfc[d]), zero)
        h_x, _ = mlstm_scan(fl(qx), fl(kx), fl(vx), fl(lix[d]), fl(lfx[d]), st)
        hx = fl(h_x) if hx is None else hx + fl(h_x)
        hc = fl(h_c) if hc is None else hc + fl(h_c)
    return hx, hc


def mlstm_out(h, o, gain):
    hn = h * lax.rsqrt(jnp.mean(h * h, axis=-1, keepdims=True) + EPS)
    b_, nh, L, dh = h.shape
    hn = jnp.swapaxes(hn, 1, 2).reshape(b_, L, nh * dh)
    return (hn * gain.astype(jnp.float32) * jax.nn.sigmoid(o.astype(jnp.float32))).astype(o.dtype)


def window_attention(q, k, v, kc, vc, sink):
    b_, S, H, dh = q.shape
    n_ctx = kc.shape[1]
    nb = S // Q_BLOCK
    span = Q_BLOCK + 2 * WINDOW
    pad = ((0, 0), (WINDOW, WINDOW), (0, 0), (0, 0))
    kp = jnp.pad(k, pad)
    vp = jnp.pad(v, pad)
    qg = q.reshape(b_, S, A_KV_HEADS, A_GROUP, dh)
    sink_g = sink.astype(jnp.float32).reshape(A_KV_HEADS, A_GROUP)[None, :, :, None, None]
    scale = dh ** -0.5

    def block(i):
        start = i * Q_BLOCK
        qb = lax.dynamic_slice_in_dim(qg, start, Q_BLOCK, axis=1)
        kb = lax.dynamic_slice_in_dim(kp, start, span, axis=1)
        vb = lax.dynamic_slice_in_dim(vp, start, span, axis=1)
        qpos = start + jnp.arange(Q_BLOCK)
        kpos = start - WINDOW + jnp.arange(span)
        mask = (jnp.abs(qpos[:, None] - kpos[None, :]) <= WINDOW) & (kpos >= 0)[None, :] & (kpos < S)[None, :]
        s_loc = jnp.einsum('bqhgd,bnhd->bhgqn', qb, kb).astype(jnp.float32) * scale
        s_loc = jnp.where(mask, s_loc, NEG_INF)
        s_ctx = jnp.einsum('bqhgd,bchd->bhgqc', qb, kc).astype(jnp.float32) * scale
        s_sink = jnp.broadcast_to(sink_g, s_ctx.shape[:-1] + (1,))
        p = jax.nn.softmax(jnp.concatenate([s_loc, s_ctx, s_sink], axis=-1), axis=-1).astype(v.dtype)
        o = (jnp.einsum('bhgqn,bnhd->bqhgd', p[..., :span], vb)
             + jnp.einsum('bhgqc,bchd->bqhgd', p[..., span:span + n_ctx], vc))
        return o.reshape(b_, Q_BLOCK, H * dh)

    out = lax.map(block, jnp.arange(nb))
    return jnp.moveaxis(out, 0, 1).reshape(b_, S, H * dh)


def context_attention(q, kc, vc, sink):
    b_, n_ctx, H, dh = q.shape
    qg = q.reshape(b_, n_ctx, A_KV_HEADS, A_GROUP, dh)
    s = jnp.einsum('bqhgd,bchd->bhgqc', qg, kc).astype(jnp.float32) * dh ** -0.5
    sink_g = sink.astype(jnp.float32).reshape(A_KV_HEADS, A_GROUP)[None, :, :, None, None]
    s_sink = jnp.broadcast_to(sink_g, s.shape[:-1] + (1,))
    p = jax.nn.softmax(jnp.concatenate([s, s_sink], axis=-1), axis=-1).astype(vc.dtype)
    o = jnp.einsum('bhgqc,bchd->bqhgd', p[..., :n_ctx], vc)
    return o.reshape(b_, n_ctx, H * dh)


def conformer_conv(u, dw_w, dw_b, ln_g, ln_b, pw_w):
    a, gate = jnp.split(u, 2, axis=-1)
    y = a * jax.nn.sigmoid(gate)
    y = lax.conv_general_dilated(y, dw_w[:, None, :].astype(y.dtype), window_strides=(1,),
                                 padding=[(CONV_K // 2, CONV_K // 2)],
                                 dimension_numbers=('NWC', 'WIO', 'NWC'),
                                 feature_group_count=C_WIDTH) + dw_b
    yf = y.astype(jnp.float32)
    mu = jnp.mean(yf, axis=-1, keepdims=True)
    var = jnp.mean(jnp.square(yf - mu), axis=-1, keepdims=True)
    y = ((yf - mu) * lax.rsqrt(var + EPS) * ln_g.astype(jnp.float32) + ln_b.astype(jnp.float32)).astype(u.dtype)
    return jax.nn.silu(y) @ pw_w


def swiglu(h, w_in, w_out):
    g, u = jnp.split(h @ w_in, 2, axis=-1)
    return (jax.nn.silu(g) * u) @ w_out


def setup_inputs(seed: int = 0) -> dict:
    key = jax.random.key(seed)
    ks = jax.random.split(key, 21)

    def nrm(k, shape, s):
        return jax.random.normal(k, shape, jnp.float32) * s

    x = nrm(ks[0], (BATCH, SEQ, D_MODEL), 1.0)
    c = nrm(ks[1], (BATCH, D_MODEL), 1.0)
    ctx = nrm(ks[2], (BATCH, CTX_LEN, D_MODEL), 1.0)
    c_ctx = nrm(ks[3], (D_MODEL,), 1.0)
    w_mod = nrm(ks[4], (DEPTH, D_MODEL, 6 * D_MODEL), 0.5 * D_MODEL ** -0.5)
    b_mod = nrm(ks[5], (DEPTH, 6 * D_MODEL), 0.02)
    norm_gain = 1.0 + nrm(ks[6], (DEPTH, 2, D_MODEL), 0.02)
    w_in = nrm(ks[7], (DEPTH, D_MODEL, IN_COLS), D_MODEL ** -0.5)
    i_bias = nrm(ks[8], (DEPTH, N_DIR, M_HEADS), 0.1)
    f_bias = jnp.linspace(3.0, 6.0, M_HEADS, dtype=jnp.float32) + nrm(ks[9], (DEPTH, N_DIR, M_HEADS), 0.1)
    mlstm_gate_bias = jnp.stack([i_bias, f_bias], axis=2)
    mlstm_head_gain = 1.0 + nrm(ks[10], (DEPTH, M_WIDTH), 0.02)
    attn_sink = nrm(ks[11], (DEPTH, A_HEADS), 0.5)
    conv_dw_w = nrm(ks[12], (DEPTH, CONV_K, C_WIDTH), CONV_K ** -0.5)
    conv_dw_b = nrm(ks[13], (DEPTH, C_WIDTH), 0.02)
    conv_ln_g = 1.0 + nrm(ks[14], (DEPTH, C_WIDTH), 0.02)
    conv_ln_b = nrm(ks[15], (DEPTH, C_WIDTH), 0.02)
    conv_pw_w = nrm(ks[16], (DEPTH, C_WIDTH, C_WIDTH), C_WIDTH ** -0.5)
    w_out = nrm(ks[17], (DEPTH, MIX_WIDTH, D_MODEL), MIX_WIDTH ** -0.5)
    w_ffn_in = nrm(ks[18], (DEPTH, D_MODEL, 2 * FFN_HIDDEN), D_MODEL ** -0.5)
    w_ffn_out = nrm(ks[19], (DEPTH, FFN_HIDDEN, D_MODEL), FFN_HIDDEN ** -0.5)
    final_gain = 1.0 + nrm(ks[20], (D_MODEL,), 0.02)
    return {'x': x, 'c': c, 'ctx': ctx, 'c_ctx': c_ctx, 'w_mod': w_mod, 'b_mod': b_mod,
            'norm_gain': norm_gain, 'w_in': w_in, 'mlstm_gate_bias': mlstm_gate_bias,
            'mlstm_head_gain': mlstm_head_gain, 'attn_sink': attn_sink, 'conv_dw_w': conv_dw_w,
            'conv_dw_b': conv_dw_b, 'conv_ln_g': conv_ln_g, 'conv_ln_b': conv_ln_b,
            'conv_pw_w': conv_pw_w, 'w_out': w_out, 'w_ffn_in': w_ffn_in, 'w_ffn_out': w_ffn_out,
            'final_gain': final_gain}


def reference(x, c, ctx, c_ctx, w_mod, b_mod, norm_gain, w_in, mlstm_gate_bias, mlstm_head_gain,
              attn_sink, conv_dw_w, conv_dw_b, conv_ln_g, conv_ln_b, conv_pw_w, w_out, w_ffn_in,
              w_ffn_out, final_gain):
    S = x.shape[1]
    n_rows = S // GRID_W
    rows = jnp.repeat(jnp.arange(n_rows), GRID_W, total_repeat_length=S)
    cols = jnp.arange(S) % GRID_W
    for l in range(DEPTH):
        last = l == DEPTH - 1
        mod_x = (jax.nn.silu(c) @ w_mod[l] + b_mod[l])[:, None, :]
        sh1x, sc1x, g1x, sh2x, sc2x, g2x = jnp.split(mod_x, 6, axis=-1)
        mod_c = jax.nn.silu(c_ctx) @ w_mod[l] + b_mod[l]
        sh1c, sc1c, g1c, sh2c, sc2c, g2c = jnp.split(mod_c, 6, axis=-1)

        hx = modulate(rmsnorm(x, norm_gain[l, 0]), sh1x, sc1x)
        hc = modulate(rmsnorm(ctx, norm_gain[l, 0]), sh1c, sc1c)
        px = split_cols(hx @ w_in[l])
        pc = split_cols(hc @ w_in[l])

        m_hx, m_hc = mlstm_bidir(mlstm_prep(px[0], px[1], px[2], px[4], mlstm_gate_bias[l]),
                                 mlstm_prep(pc[0], pc[1], pc[2], pc[4], mlstm_gate_bias[l]))
        m_x = mlstm_out(m_hx, px[3], mlstm_head_gain[l])

        q_x = rope_2d(heads(px[5], A_HEADS), rows, cols)
        k_x = rope_2d(heads(px[6], A_KV_HEADS), rows, cols)
        v_x = heads(px[7], A_KV_HEADS)
        k_c = heads(pc[6], A_KV_HEADS)
        v_c = heads(pc[7], A_KV_HEADS)
        a_x = window_attention(q_x, k_x, v_x, k_c, v_c, attn_sink[l])

        c_x = conformer_conv(px[8], conv_dw_w[l], conv_dw_b[l], conv_ln_g[l], conv_ln_b[l], conv_pw_w[l])

        x = x + g1x * (jnp.concatenate([m_x, a_x, c_x], axis=-1) @ w_out[l])
        x = x + g2x * swiglu(modulate(rmsnorm(x, norm_gain[l, 1]), sh2x, sc2x), w_ffn_in[l], w_ffn_out[l])

        if not last:
            m_c = mlstm_out(m_hc, pc[3], mlstm_head_gain[l])
            a_c = context_attention(heads(pc[5], A_HEADS), k_c, v_c, attn_sink[l])
            c_c = conformer_conv(pc[8], conv_dw_w[l], conv_dw_b[l], conv_ln_g[l], conv_ln_b[l], conv_pw_w[l])
            ctx = ctx + g1c * (jnp.concatenate([m_c, a_c, c_c], axis=-1) @ w_out[l])
            ctx = ctx + g2c * swiglu(modulate(rmsnorm(ctx, norm_gain[l, 1]), sh2c, sc2c), w_ffn_in[l], w_ffn_out[l])
    return rmsnorm(x, final_gain)
```

```python
import contextlib
import numpy as np
import concourse.bass as bass
import concourse.mybir as mybir
from concourse.bass_utils import run_bass_kernel_spmd

F32 = mybir.dt.float32
BF16 = mybir.dt.bfloat16
AF = mybir.ActivationFunctionType
ALU = mybir.AluOpType
AX = mybir.AxisListType

D = 1024
CTX = 256
DEPTH = 2
HID = 2816
EPS = 1e-6
NFM = 20
TM0 = NFM * 128
NTM = 912
NC_EXT = TM0 + NTM


class Buf:
    def __init__(self, name, t):
        self.name = name
        self.t = t
        self.writers = []
        self.readers = []
        self.epoch_deps = []
        self.sem = None
        self.excl = False

    def __getitem__(self, k):
        return self.t[k]


class Sem:
    def __init__(self, h, key):
        self.h = h
        self.key = key
        self.cnt = 0


class KB:
    def __init__(self, nc, es):
        self.nc = nc
        self.es = es
        self.eng = {"pe": nc.tensor, "act": nc.scalar, "dve": nc.vector, "pool": nc.gpsimd, "sp": nc.sync}
        self.esem = {}
        for e in ("pe", "act", "dve", "pool"):
            self.esem[e] = Sem(es.enter_context(nc.semaphore("se_" + e)), "e_" + e)
        self.dpool = [Sem(es.enter_context(nc.semaphore("sd%d" % i)), "d%d" % i) for i in range(72)]
        self.dused = []
        self.waited = {e: {} for e in self.eng}
        self.uid = 0
        self.stacks = [es]
        self.outstanding = {}
        self.phase_bufs = [[]]

    def push(self):
        s = contextlib.ExitStack()
        s.__enter__()
        self.stacks.append(s)
        self.phase_bufs.append([])

    def pop(self):
        self.barrier()
        for b in self.phase_bufs.pop():
            if b.sem is not None:
                self.dpool.append(b.sem)
                b.sem = None
        s = self.stacks.pop()
        s.__exit__(None, None, None)

    def sb(self, name, shape, dt):
        self.uid += 1
        h = self.stacks[-1].enter_context(self.nc.sbuf_tensor("%s_%d" % (name, self.uid), list(shape), dt))
        b = Buf(name, h)
        self.phase_bufs[-1].append(b)
        return b

    def ring(self, name, shape, dt, n):
        return Ring([self.sb("%s%d" % (name, i), shape, dt) for i in range(n)])

    def dram(self, name, shape, dt, kind="Internal"):
        t = self.nc.dram_tensor(name, list(shape), dt, kind=kind)
        return Buf(name, t.ap())

    def _deps(self, reads, writes, nowaw, own=None):
        deps = {}

        def add(evs):
            for (s, v) in evs:
                if s.key not in deps or deps[s.key][1] < v:
                    deps[s.key] = (s, v)

        for b in reads:
            add(b.writers)
            if b.excl:
                add([ev for ev in b.readers if ev[0].key != own])
        for b in writes:
            if nowaw and not b.readers and b.writers:
                add(b.epoch_deps)
            else:
                add(b.readers)
                add(b.writers)
        return deps

    def _emit_waits(self, engine, deps):
        e = self.eng[engine]
        w = self.waited[engine]
        for key, (s, v) in deps.items():
            if engine == "pe" and key == "e_pe":
                continue
            if w.get(key, 0) >= v:
                continue
            e.wait_ge(s.h, v)
            w[key] = v

    def _commit(self, ev, reads, writes, nowaw):
        for b in reads:
            b.readers.append(ev)
        for b in writes:
            if nowaw and not b.readers and b.writers:
                b.writers.append(ev)
            else:
                b.epoch_deps = b.readers + b.writers
                b.writers = [ev]
                b.readers = []
        self.outstanding[ev[0].key] = ev

    def op(self, engine, fn, reads=(), writes=(), nowaw=False):
        deps = self._deps(reads, writes, nowaw, own="e_" + engine)
        self._emit_waits(engine, deps)
        inst = fn(self.eng[engine])
        s = self.esem[engine]
        s.cnt += 1
        inst.then_inc(s.h, 1)
        self._commit((s, s.cnt), reads, writes, nowaw)
        return inst

    def dma(self, out, in_, reads, writes, sembuf, q="sp", nowaw=False):
        deps = self._deps(reads, writes, nowaw)
        self._emit_waits(q, deps)
        if sembuf.sem is None:
            sembuf.sem = self.dpool.pop()
        s = sembuf.sem
        inst = self.eng[q].dma_start(out=out, in_=in_)
        s.cnt += 16
        inst.then_inc(s.h, 16)
        self._commit((s, s.cnt), reads, writes, nowaw)
        return inst

    def barrier(self):
        evs = dict(self.outstanding)
        for engine in self.eng:
            self._emit_waits(engine, evs)

    def final_wait(self):
        self._emit_waits("sp", dict(self.outstanding))


class Ring:
    def __init__(self, bufs):
        self.bufs = bufs
        self.i = 0

    def next(self):
        b = self.bufs[self.i % len(self.bufs)]
        self.i += 1
        return b


def _swap(dd):
    return dd + 16 if (dd % 32) < 16 else dd - 16


def ext_cols():
    cols = []
    cols += list(range(0, 256))
    cols += list(range(256, 512))
    cols += [1040 + i for i in range(512)]
    cols += [1040 + (i // 64) * 64 + _swap(i % 64) for i in range(512)]
    for g in range(2):
        cols += [1552 + g * 64 + dd for dd in range(64)] * 2
    for g in range(2):
        cols += [1552 + g * 64 + _swap(dd) for dd in range(64)] * 2
    cols += list(range(1808, 2064))
    cols += list(range(2064, 2320))
    assert len(cols) == TM0
    cols += list(range(256, 512)) + list(range(512, 768)) + list(range(768, 1024))
    cols += list(range(1024, 1040)) + list(range(1680, 1808))
    assert len(cols) == NC_EXT
    return np.array(cols)


def pmaj(v, nk):
    sh = v.shape[:-1]
    a = v.reshape(sh + (nk, 128))
    return np.ascontiguousarray(np.moveaxis(a, -1, 0))


def rope_tables(S):
    half = 16
    freqs = (np.float32(10000.0) ** (-np.arange(half, dtype=np.float32) / np.float32(half))).astype(np.float32)
    t = np.arange(S)
    rows = (t // 64).astype(np.float32)
    colsp = (t % 64).astype(np.float32)
    cos_t = np.zeros((128, S), np.float32)
    sin_t = np.zeros((128, S), np.float32)
    for p in range(128):
        dd = p % 64
        pos = rows if dd < 32 else colsp
        d2 = dd % 32
        i = d2 % 16
        ang = (pos * freqs[i]).astype(np.float32)
        cos_t[p] = np.cos(ang)
        sgn = -1.0 if d2 < 16 else 1.0
        sin_t[p] = sgn * np.sin(ang)
    return cos_t, sin_t


def prep_shared(inp, S):
    f = lambda a: np.ascontiguousarray(np.asarray(a, dtype=np.float32))
    sh = {}
    cols = ext_cols()
    w_in = f(inp["w_in"])
    sh["w_ext"] = np.ascontiguousarray(w_in[:, :, cols].reshape(DEPTH, 8, 128, NC_EXT).transpose(0, 2, 1, 3))
    sh["w_mod"] = np.ascontiguousarray(f(inp["w_mod"]).reshape(DEPTH, 8, 128, 6 * D).transpose(0, 2, 1, 3))
    sh["b_mod"] = pmaj(f(inp["b_mod"]), 48)
    sh["ngain"] = pmaj(f(inp["norm_gain"]), 8)
    sh["fgain"] = pmaj(f(inp["final_gain"]), 8)
    sh["gbias"] = f(inp["mlstm_gate_bias"]).reshape(DEPTH, 16)
    sh["hgain"] = f(inp["mlstm_head_gain"])
    sh["sink"] = f(inp["attn_sink"])
    sh["dw_w"] = np.ascontiguousarray(f(inp["conv_dw_w"]).reshape(DEPTH, 31, 2, 128).transpose(0, 3, 2, 1))
    sh["dw_b"] = pmaj(f(inp["conv_dw_b"]), 2)
    sh["ln_g"] = pmaj(f(inp["conv_ln_g"]), 2)
    sh["ln_b"] = pmaj(f(inp["conv_ln_b"]), 2)
    sh["pw_w"] = np.ascontiguousarray(f(inp["conv_pw_w"]).reshape(DEPTH, 2, 128, 256).transpose(0, 2, 1, 3))
    sh["w_out"] = np.ascontiguousarray(f(inp["w_out"]).reshape(DEPTH, 8, 128, D).transpose(0, 2, 1, 3))
    sh["w_fi"] = np.ascontiguousarray(f(inp["w_ffn_in"]).reshape(DEPTH, 8, 128, 2 * HID).transpose(0, 2, 1, 3))
    sh["w_fo"] = np.ascontiguousarray(f(inp["w_ffn_out"]).reshape(DEPTH, 22, 128, D).transpose(0, 2, 1, 3))
    ident = np.eye(128, dtype=np.float32)
    s_i = np.arange(128)[:, None]
    t_i = np.arange(128)[None, :]
    consts = np.zeros((128, 4, 128), np.float32)
    consts[:, 0] = ident
    consts[:, 1] = (s_i <= t_i)
    consts[:, 2] = (s_i >= t_i)
    consts[:, 3] = 1.0
    sh["consts"] = consts
    cos_t, sin_t = rope_tables(S)
    sh["cos_t"] = cos_t
    sh["sin_t"] = sin_t
    return sh


def prep_core(inp, b, S):
    f = lambda a: np.ascontiguousarray(np.asarray(a, dtype=np.float32))
    pc = {}
    pc["x"] = f(inp["x"][b, :S])
    pc["ctx"] = f(inp["ctx"][b])
    cc = np.stack([f(inp["c"][b]), f(inp["c_ctx"])], axis=0)
    pc["cc"] = np.ascontiguousarray(cc.reshape(2, 8, 128).transpose(2, 1, 0))
    return pc


INPUT_SHAPES = lambda S: {
    "x": [S, D], "ctx": [CTX, D], "cc": [128, 8, 2],
    "w_ext": [DEPTH, 128, 8, NC_EXT], "w_mod": [DEPTH, 128, 8, 6 * D], "b_mod": [128, DEPTH, 48],
    "ngain": [128, DEPTH, 2, 8], "fgain": [128, 8], "gbias": [DEPTH, 16], "hgain": [DEPTH, 256],
    "sink": [DEPTH, 8], "dw_w": [DEPTH, 128, 2, 31], "dw_b": [128, DEPTH, 2], "ln_g": [128, DEPTH, 2],
    "ln_b": [128, DEPTH, 2], "pw_w": [DEPTH, 128, 2, 256], "w_out": [DEPTH, 128, 8, D],
    "w_fi": [DEPTH, 128, 8, 2 * HID], "w_fo": [DEPTH, 128, 22, D], "consts": [128, 4, 128],
    "cos_t": [128, S], "sin_t": [128, S],
}


class Prog:
    def __init__(self, S, dbg=(), upto=None, nlayers=DEPTH):
        self.S = S
        self.L = CTX + S
        self.NB = self.L // 128
        self.dbg = set(dbg)
        self.upto = upto
        self.nlayers = nlayers
        self.tiles = [(0, CTX, "c")] + [(CTX + 512 * i, 512, "x") for i in range(S // 512)]

    def mm(self, out, lhsT, rhs, start, stop, reads, writes, nowaw=False):
        return self.kb.op("pe", lambda e: e.matmul(out, lhsT, rhs, start=start, stop=stop), reads, writes, nowaw)

    def act(self, out, in_, func, reads, writes, nowaw=False, **kw):
        return self.kb.op("act", lambda e: e.activation(out=out, in_=in_, func=func, **kw), reads, writes, nowaw)

    def cp(self, eng, out, in_, reads, writes, nowaw=False):
        if eng == "act":
            return self.kb.op("act", lambda e: e.copy(out=out, in_=in_), reads, writes, nowaw)
        return self.kb.op(eng, lambda e: e.tensor_copy(out=out, in_=in_), reads, writes, nowaw)

    def tt(self, eng, out, in0, in1, op, reads, writes, nowaw=False):
        return self.kb.op(eng, lambda e: e.tensor_tensor(out=out, in0=in0, in1=in1, op=op), reads, writes, nowaw)

    def stt(self, eng, out, in0, scalar, in1, op0, op1, reads, writes, nowaw=False):
        return self.kb.op(eng, lambda e: e.scalar_tensor_tensor(out=out, in0=in0, scalar=scalar, in1=in1, op0=op0, op1=op1),
                          reads, writes, nowaw)

    def ts(self, eng, out, in0, s1, s2, op0, op1, reads, writes, nowaw=False):
        return self.kb.op(eng, lambda e: e.tensor_scalar(out=out, in0=in0, scalar1=s1, scalar2=s2, op0=op0, op1=op1),
                          reads, writes, nowaw)

    def dram(self, name, shape, dt):
        kind = "ExternalOutput" if name in self.dbg else "Internal"
        return self.kb.dram(name, shape, dt, kind=kind)

    def build(self):
        nc = bass.Bass("TRN2", target_bir_lowering=False)
        self.nc = nc
        S, L, NB = self.S, self.L, self.NB
        es = contextlib.ExitStack()
        with es:
            kb = KB(nc, es)
            self.kb = kb
            self.din = {k: kb.dram(k, shp, F32, kind="ExternalInput") for k, shp in INPUT_SHAPES(S).items()}
            self.y = kb.dram("y", [S, D], F32, kind="ExternalOutput")
            self.xT = [self.dram("xTa", [D, L], F32), self.dram("xTb", [D, L], F32)]
            self.mqT = self.dram("mqT", [256, L], BF16)
            self.mkT = self.dram("mkT", [256, L], BF16)
            self.mk_tm = self.dram("mk_tm", [128, NB, 256], BF16)
            self.mv_tm = self.dram("mv_tm", [128, NB, 260], BF16)
            self.mo_tm = self.dram("mo_tm", [128, NB, 256], F32)
            self.aqT = self.dram("aqT", [512, L], BF16)
            self.akT = self.dram("akT", [256, L], BF16)
            self.av_tm = self.dram("av_tm", [128, NB, 130], BF16)
            self.yT = self.dram("yT", [256, L], BF16)
            self.mixT = self.dram("mixT", [D, L], BF16)
            self.aT = self.dram("aT", [HID, L], BF16)
            if "hT" in self.dbg:
                self.dbg_hT = self.dram("hT", [D, L], BF16)
            self.ps = []
            for i in range(8):
                h = es.enter_context(nc.psum_tensor("psb%d" % i, [128, 512], F32))
                self.ps.append(Buf("ps%d" % i, h))
                self.ps[-1].excl = True
            self.cf = kb.sb("cf", [128, 4, 128], F32)
            self.cb = kb.sb("cb", [128, 4, 128], BF16)
            self.ones256 = kb.sb("o256", [128, 128], BF16)
            self.mask4 = kb.sb("mask4", [128, 2, 512], BF16)
            self.cc = kb.sb("cc", [128, 8, 2], F32)
            self.small = {}
            for nm in ("b_mod", "ngain", "fgain", "dw_b", "ln_g", "ln_b"):
                shp = INPUT_SHAPES(S)[nm]
                self.small[nm] = kb.sb(nm, shp, F32)
            self.modv = kb.sb("modv", [128, DEPTH, 2, 48], F32)
            self.gm = kb.sb("gm", [128, DEPTH, 2, 2, 8], F32)
            self.gates = kb.sb("gates", [128, NB, 16], F32)
            self.epsb = kb.sb("epsb", [128, 2], F32)
            self.init_consts()
            self.run()
            kb.final_wait()
        return nc

    def init_consts(self):
        kb = self.kb
        kb.dma(self.cf[:], self.din["consts"][:, :, :], [self.din["consts"]], [self.cf], self.cf)
        kb.dma(self.cc[:], self.din["cc"][:, :, :], [self.din["cc"]], [self.cc], self.cc)
        for nm, b in self.small.items():
            src = self.din[nm]
            idx = tuple(slice(None) for _ in INPUT_SHAPES(self.S)[nm])
            kb.dma(b[idx], src[idx], [src], [b], b)
        self.cp("dve", self.cb[:], self.cf[:], [self.cf], [self.cb])
        self.kb.op("dve", lambda e: e.tensor_scalar_mul(out=self.ones256[:], in0=self.cf[:, 3, :], scalar1=1.0 / 256.0),
                   [self.cf], [self.ones256])
        self.kb.op("dve", lambda e: e.memset(self.epsb[:], EPS), [], [self.epsb])
        for i in range(4):
            self.cp("dve", self.mask4[:, 0, i * 128:(i + 1) * 128], self.cf[:, 1, :], [self.cf], [self.mask4], nowaw=i > 0)
            self.cp("dve", self.mask4[:, 1, i * 128:(i + 1) * 128], self.cf[:, 2, :], [self.cf], [self.mask4], nowaw=True)

    @property
    def ident_f(self):
        return self.cf[:, 0, :]

    @property
    def ones_b(self):
        return self.cb[:, 3, :]

    def stop(self, name):
        return self.upto == name

    def run(self):
        self.phase_p0()
        if self.stop("p0"):
            return
        cur = 0
        for l in range(self.nlayers):
            last = l == DEPTH - 1
            self.phase_mod(l)
            if self.stop("mod"):
                return
            self.phase_p1(l, self.xT[cur], last)
            if self.stop("p1"):
                return
            self.phase_mlstm(l, last)
            if self.stop("mlstm"):
                return
            self.phase_attn(l, last)
            if self.stop("attn"):
                return
            self.phase_conv(l, last)
            if self.stop("conv"):
                return
            self.phase_p3(l, self.xT[cur], self.xT[1 - cur], last)
            cur = 1 - cur
            if self.stop("p3"):
                return
            self.phase_f1(l, self.xT[cur], last)
            if self.stop("f1"):
                return
            self.phase_f2(l, self.xT[cur], self.xT[1 - cur], last)
            cur = 1 - cur
            if self.stop("f2"):
                return
        self.phase_final(self.xT[cur])

    def phase_p0(self):
        kb = self.kb
        kb.push()
        xin = kb.ring("xin", [128, D], F32, 3)
        xst = kb.ring("xst", [128, 8, 128], F32, 3)
        xTv = self.xT[0].t.rearrange("(k p) t -> p k t", p=128)
        psr = Ring(self.ps)

        def load(blk):
            t = xin.next()
            if blk < 2:
                src, sb_ = self.din["ctx"][blk * 128:(blk + 1) * 128, :], self.din["ctx"]
            else:
                src, sb_ = self.din["x"][(blk - 2) * 128:(blk - 1) * 128, :], self.din["x"]
            kb.dma(t[:], src, [sb_], [t], t)
            return t

        pend = {0: load(0)}
        if self.NB > 1:
            pend[1] = load(1)
        for blk in range(self.NB):
            if blk + 2 < self.NB:
                pend[blk + 2] = load(blk + 2)
            t = pend.pop(blk)
            o = xst.next()
            for half in range(2):
                ps = psr.next()
                for kk in range(4):
                    k = half * 4 + kk
                    kb.op("pe", lambda e: e.transpose(ps[:, kk * 128:(kk + 1) * 128], t[:, k * 128:(k + 1) * 128], self.ident_f),
                          [t, self.cf], [ps], nowaw=kk > 0)
                self.cp("act" if half == 0 else "dve", o[:, half * 4:(half + 1) * 4, :],
                        ps[:, :].rearrange("p (k t) -> p k t", k=4), [ps], [o], nowaw=half > 0)
            kb.dma(xTv[:, :, blk * 128:(blk + 1) * 128], o[:], [o], [self.xT[0]], o, nowaw=True)
        kb.pop()

    def phase_mod(self, l):
        kb = self.kb
        kb.push()
        wst = kb.ring("wm", [128, 8, 512], F32, 2)
        sc = kb.sb("silu_c", [128, 8, 2], F32)
        self.act(sc[:], self.cc[:], AF.Silu, [self.cc], [sc])
        ps = self.ps[0]
        wm = self.din["w_mod"]
        first = True
        for piece in range(12):
            t = wst.next()
            kb.dma(t[:], wm[l, :, :, piece * 512:(piece + 1) * 512], [wm], [t], t)
            for cq in range(4):
                j = piece * 4 + cq
                for k in range(8):
                    self.mm(ps[:, 2 * j:2 * j + 2], t[:, k, cq * 128:(cq + 1) * 128], sc[:, k, :], k == 0, k == 7,
                            [t, sc], [ps], nowaw=not first)
                    first = False
        psv = ps[:, 0:96].rearrange("p (j c) -> p j c", c=2)
        bm = self.small["b_mod"]
        for v in range(2):
            self.tt("dve", self.modv[:, l, v, :], psv[:, :, v], bm[:, l, :], ALU.add, [ps, bm], [self.modv], nowaw=True)
        ng = self.small["ngain"]
        for v in range(2):
            for i in range(2):
                scv = self.modv[:, l, v, (3 * i + 1) * 8:(3 * i + 2) * 8]
                self.stt("dve", self.gm[:, l, v, i, :], scv, 1.0, ng[:, l, i, :], ALU.add, ALU.mult,
                         [self.modv, ng], [self.gm], nowaw=True)
        kb.pop()

    def modcol(self, l, v, m, k):
        return self.modv[:, l, v, m * 8 + k:m * 8 + k + 1]

    def load_cast(self, dst, src_ap_fn, src_buf, ncols, nk, engines=("dve", "pool", "act")):
        kb = self.kb
        kb.push()
        st = kb.ring("wst", [128, nk, 512], F32, 2)
        i = 0
        for c0 in range(0, ncols, 512):
            cw = min(512, ncols - c0)
            t = st.next()
            kb.dma(t[:, :, :cw], src_ap_fn(c0, cw), [src_buf], [t], t)
            eng = engines[i % len(engines)]
            self.cp(eng, dst[:, :, c0:c0 + cw], t[:, :, :cw], [t], [dst], nowaw=i > 0)
            i += 1
        kb.pop()

    def norm_tile(self, xt, w, gm_fn, sh_fn, hT, sqr, tmpr, rstd, psb, plain=False):
        for k in range(8):
            sq = sqr.next()
            self.act(sq[:, :w], xt[:, k, :w], AF.Square, [xt], [sq])
            self.mm(psb[:, :w], self.ones_b, sq[:, :w], k == 0, k == 7, [sq, self.cb], [psb], nowaw=k > 0)
        self.act(rstd[:, :w], psb[:, :w], AF.Sqrt, [psb], [rstd], scale=1.0 / D, bias=self.epsb[:, 0:1])
        self.kb.op("dve", lambda e: e.reciprocal(out=rstd[:, :w], in_=rstd[:, :w]), [rstd], [rstd])
        for k in range(8):
            if plain:
                self.stt("dve", hT[:, k, :w], xt[:, k, :w], gm_fn(k), rstd[:, :w], ALU.mult, ALU.mult,
                         [xt, rstd, self.small["fgain"]], [hT], nowaw=k > 0)
                continue
            tmp = tmpr.next()
            self.stt("dve", tmp[:, :w], xt[:, k, :w], gm_fn(k), rstd[:, :w], ALU.mult, ALU.mult,
                     [xt, rstd, self.gm], [tmp])
            self.act(hT[:, k, :w], tmp[:, :w], AF.Identity, [tmp, self.modv], [hT], nowaw=k > 0, bias=sh_fn(k), scale=1.0)

    def phase_p1(self, l, xsrc, last):
        kb = self.kb
        S = self.S
        kb.push()
        wext = kb.sb("wext", [128, 8, NC_EXT], BF16)
        wsrc = self.din["w_ext"]
        self.load_cast(wext, lambda c0, cw: wsrc[l, :, :, c0:c0 + cw], wsrc, NC_EXT, 8)
        import os
        P1S = int(os.environ.get("P1S", "99"))
        if P1S < 1:
            kb.pop(); return
        xr = kb.ring("xt", [128, 8, 512], F32, 2)
        hr = kb.ring("hT", [128, 8, 512], BF16, 2)
        sqr = kb.ring("sq", [128, 512], BF16, 2)
        tmpr = kb.ring("tmp", [128, 512], F32, 3)
        rstd_r = kb.ring("rstd", [128, 512], F32, 2)
        cosr = kb.ring("cos", [128, 512], F32, 2)
        sinr = kb.ring("sin", [128, 512], F32, 2)
        stg = kb.ring("stg", [128, 512], BF16, 6)
        r1 = kb.ring("r1", [128, 512], F32, 3)
        r2 = kb.ring("r2", [128, 512], F32, 3)
        sk = kb.ring("sk", [128, 256], BF16, 3)
        sv = kb.ring("sv", [128, 4, 65], BF16, 3)
        so = kb.ring("so", [128, 256], F32, 3)
        sa = kb.ring("sa", [128, 2, 65], BF16, 3)
        for b in sv.bufs + sa.bufs:
            kb.op("pool", lambda e: e.memset(b[:], 1.0), [], [b])
        psr = Ring(self.ps[0:6])
        psn = Ring(self.ps[6:8])
        xv = xsrc.t.rearrange("(k p) t -> p k t", p=128)

        def load(ti):
            e0, w, kind = self.tiles[ti]
            xt = xr.next()
            kb.dma(xt[:, :, :w], xv[:, :, e0:e0 + w], [xsrc], [xt], xt)
            cs = sn = None
            if kind == "x":
                cs, sn = cosr.next(), sinr.next()
                t0 = e0 - CTX
                kb.dma(cs[:, :w], self.din["cos_t"][:, t0:t0 + w], [self.din["cos_t"]], [cs], cs)
                kb.dma(sn[:, :w], self.din["sin_t"][:, t0:t0 + w], [self.din["sin_t"]], [sn], sn)
            return xt, cs, sn

        def fm(m, hT, w):
            ps = psr.next()
            for k in range(8):
                self.mm(ps[:, :w], wext[:, k, m * 128:(m + 1) * 128], hT[:, k, :w], k == 0, k == 7, [wext, hT], [ps], nowaw=k > 0)
            return ps

        def store_fm(dst, row0, e0, w, st):
            kb.dma(dst[row0:row0 + 128, e0:e0 + w], st[:, :w], [st], [dst], st, nowaw=True)

        pend = {0: load(0)}
        nt = len(self.tiles)
        for ti in range(nt):
            if ti + 1 < nt:
                pend[ti + 1] = load(ti + 1)
            e0, w, kind = self.tiles[ti]
            v = 0 if kind == "x" else 1
            xt, cs, sn = pend.pop(ti)
            hT = hr.next()
            self.norm_tile(xt, w, lambda k: self.gm[:, l, v, 0, k:k + 1], lambda k: self.modcol(l, v, 0, k),
                           hT, sqr, tmpr, rstd_r.next(), psn.next())
            if "hT" in self.dbg and l == 0:
                kb.dma(self.dbg_hT.t.rearrange("(k p) t -> p k t", p=128)[:, :, e0:e0 + w], hT[:, :, :w], [hT], [self.dbg_hT], hT, nowaw=True)
            if P1S < 2:
                continue
            for m in range(4):
                ps = fm(m, hT, w)
                st = stg.next()
                self.cp("act", st[:, :w], ps[:, :w], [ps], [st])
                store_fm(self.mqT if m < 2 else self.mkT, (m % 2) * 128, e0, w, st)
            if P1S < 3:
                continue
            need_ctx_q = not last
            for j in range(4):
                if kind == "c" and not need_ctx_q:
                    continue
                ps = fm(4 + j, hT, w)
                st = stg.next()
                if kind == "x":
                    ps2 = fm(8 + j, hT, w)
                    t1, t2 = r1.next(), r2.next()
                    self.tt("dve", t1[:, :w], ps[:, :w], cs[:, :w], ALU.mult, [ps, cs], [t1])
                    self.tt("dve", t2[:, :w], ps2[:, :w], sn[:, :w], ALU.mult, [ps2, sn], [t2])
                    self.tt("pool", st[:, :w], t1[:, :w], t2[:, :w], ALU.add, [t1, t2], [st])
                else:
                    self.cp("act", st[:, :w], ps[:, :w], [ps], [st])
                store_fm(self.aqT, j * 128, e0, w, st)
            for g in range(2):
                ps = fm(12 + g, hT, w)
                st = stg.next()
                if kind == "x":
                    ps2 = fm(14 + g, hT, w)
                    t1, t2 = r1.next(), r2.next()
                    self.tt("dve", t1[:, :w], ps[:, :w], cs[:, :w], ALU.mult, [ps, cs], [t1])
                    self.tt("dve", t2[:, :w], ps2[:, :w], sn[:, :w], ALU.mult, [ps2, sn], [t2])
                    self.tt("pool", st[:, :w], t1[:, :w], t2[:, :w], ALU.add, [t1, t2], [st])
                else:
                    self.cp("act", st[:, :w], ps[:, :w], [ps], [st])
                store_fm(self.akT, g * 128, e0, w, st)
            if P1S < 4:
                continue
            if not (kind == "c" and last):
                for ch in range(2):
                    psv_ = fm(16 + ch, hT, w)
                    psg_ = fm(18 + ch, hT, w)
                    sg = r1.next()
                    self.act(sg[:, :w], psg_[:, :w], AF.Sigmoid, [psg_], [sg])
                    st = stg.next()
                    self.tt("dve", st[:, :w], psv_[:, :w], sg[:, :w], ALU.mult, [psv_, sg], [st])
                    store_fm(self.yT, ch * 128, e0, w, st)
            if P1S < 5:
                continue
            for j in range(w // 128):
                blk = e0 // 128 + j
                psA, psB = psr.next(), psr.next()
                for k in range(8):
                    self.mm(psA[:, 0:512], hT[:, k, j * 128:(j + 1) * 128], wext[:, k, TM0:TM0 + 512], k == 0, k == 7,
                            [wext, hT], [psA], nowaw=k > 0)
                for k in range(8):
                    self.mm(psB[:, 0:400], hT[:, k, j * 128:(j + 1) * 128], wext[:, k, TM0 + 512:TM0 + 912], k == 0, k == 7,
                            [wext, hT], [psB], nowaw=k > 0)
                P1T = int(os.environ.get("P1T", "7"))
                if not (P1T & 2):
                    continue
                k_, v_, o_, a_ = sk.next(), sv.next(), so.next(), sa.next()
                self.cp("act", k_[:], psA[:, 0:256], [psA], [k_])
                self.cp("dve", v_[:, :, 0:64], psA[:, 256:512].rearrange("p (h d) -> p h d", h=4), [psA], [v_])
                self.cp("act", o_[:], psB[:, 0:256], [psB], [o_])
                self.cp("dve", self.gates[:, blk, :], psB[:, 256:272], [psB], [self.gates], nowaw=True)
                self.cp("dve", a_[:, :, 0:64], psB[:, 272:400].rearrange("p (h d) -> p h d", h=2), [psB], [a_])
                if not (P1T & 4):
                    continue
                kb.dma(self.mk_tm[:, blk, :], k_[:], [k_], [self.mk_tm], k_, nowaw=True)
                kb.dma(self.mv_tm[:, blk, :], v_[:].rearrange("p h d -> p (h d)"), [v_], [self.mv_tm], v_, nowaw=True)
                kb.dma(self.mo_tm[:, blk, :], o_[:], [o_], [self.mo_tm], o_, nowaw=True)
                kb.dma(self.av_tm[:, blk, :], a_[:].rearrange("p h d -> p (h d)"), [a_], [self.av_tm], a_, nowaw=True)
        kb.pop()

    def phase_mlstm(self, l, last):
        kb = self.kb
        NB = self.NB
        kb.push()
        gb = kb.sb("gb", [128, 16], F32)
        kb.dma(gb[:], self.din["gbias"][l:l + 1, :].partition_broadcast(128), [self.din["gbias"]], [gb], gb)
        hg = kb.sb("hg", [128, 256], F32)
        kb.dma(hg[:], self.din["hgain"][l:l + 1, :].partition_broadcast(128), [self.din["hgain"]], [hg], hg)
        G = self.gates
        gv = G[:].rearrange("p n (d g h) -> p n d g h", d=2, g=2)
        gbv = gb[:].rearrange("p (d g h) -> p d g h", d=2, g=2)
        lf = kb.sb("lf", [128, 2, NB, 4], F32)
        li = kb.sb("li", [128, 2, NB, 4], F32)
        Bc = kb.sb("Bc", [128, 2, NB, 4], F32)
        BT = kb.sb("BT", [128, 2, NB, 4], F32)
        A_ = kb.sb("A_", [128, 2, NB, 4], F32)
        E_ = kb.sb("E_", [128, 2, NB, 4], F32)
        G_ = kb.sb("G_", [128, 2, NB, 4], F32)
        DEC = kb.sb("DEC", [128, 2, NB, 2], F32)
        for d in range(2):
            bb = gbv[:, d, :, :].unsqueeze(1).to_broadcast([128, NB, 2, 4])
            self.tt("dve", li[:, d, :, :], gv[:, :, d, 0, :], gbv[:, d, 0, :].unsqueeze(1).to_broadcast([128, NB, 4]),
                    ALU.add, [G, gb], [li], nowaw=d > 0)
            self.tt("dve", lf[:, d, :, :], gv[:, :, d, 1, :], gbv[:, d, 1, :].unsqueeze(1).to_broadcast([128, NB, 4]),
                    ALU.add, [G, gb], [lf], nowaw=d > 0)
        fl = lambda b_: b_[:].rearrange("p a n h -> p (a n h)")
        self.act(fl(lf), fl(lf), AF.Exp, [lf], [lf], scale=-1.0)
        self.act(fl(lf), fl(lf), AF.Ln, [lf, self.cf], [lf], bias=self.cf[:, 3, 0:1], scale=1.0)
        kb.op("dve", lambda e: e.tensor_scalar_mul(out=fl(lf), in0=fl(lf), scalar1=-1.0), [lf], [lf])
        NC4 = NB * 4
        assert NC4 <= 512
        psb, pst = self.ps[0], self.ps[1]
        for d in range(2):
            tri = self.cf[:, 1 + d, :]
            rhs = lf[:, d, :, :].rearrange("p n h -> p (n h)")
            self.mm(psb[:, 0:NC4], tri, rhs, True, True, [self.cf, lf], [psb])
            self.mm(pst[:, 0:NC4], self.cf[:, 3, :], rhs, True, True, [self.cf, lf], [pst])
            self.cp("dve", Bc[:, d, :, :].rearrange("p n h -> p (n h)"), psb[:, 0:NC4], [psb], [Bc], nowaw=d > 0)
            self.cp("dve", BT[:, d, :, :].rearrange("p n h -> p (n h)"), pst[:, 0:NC4], [pst], [BT], nowaw=d > 0)
        self.tt("dve", fl(A_), fl(li), fl(Bc), ALU.subtract, [li, Bc], [A_])
        self.act(fl(A_), fl(A_), AF.Exp, [A_], [A_])
        kb.op("dve", lambda e: e.tensor_scalar_mul(out=fl(A_), in0=fl(A_), scalar1=0.125), [A_], [A_])
        self.act(fl(E_), fl(Bc), AF.Exp, [Bc], [E_])
        self.act(fl(BT), fl(BT), AF.Exp, [BT], [BT])
        self.tt("dve", fl(G_), fl(A_), fl(BT), ALU.mult, [A_, BT], [G_])
        for d in range(2):
            for pr in range(2):
                self.cp("dve", DEC[0:64, d, :, pr], BT[0:64, d, :, 2 * pr], [BT], [DEC], nowaw=(d + pr) > 0)
                self.cp("dve", DEC[64:128, d, :, pr], BT[64:128, d, :, 2 * pr + 1], [BT], [DEC], nowaw=True)
        import os
        M2S = int(os.environ.get("M2S", "99"))
        if M2S < 1:
            kb.pop(); return
        hsum = kb.sb("hsum", [128, NB, 256], F32)
        hsb = [Buf("hs%d" % i, hsum.t) for i in range(NB)]
        Cst = [kb.sb("Cst%d" % d, [128, 2, 130], F32) for d in range(2)]
        Cbf = [kb.sb("Cbf%d" % d, [128, 2, 130], BF16) for d in range(2)]
        for d in range(2):
            kb.op("pool", lambda e: e.memset(Cst[d][:], 0.0), [], [Cst[d]])
            kb.op("pool", lambda e: e.memset(Cbf[d][:], 0.0), [], [Cbf[d]])
        qr = kb.ring("qT", [128, 2, 128], BF16, 7)
        kr = kb.ring("kT", [128, 2, 128], BF16, 7)
        ktr = kb.ring("ktm", [128, 256], BF16, 7)
        vr = kb.ring("vtm", [128, 260], BF16, 7)
        orr = kb.ring("otm", [128, 256], F32, 7)
        pTr = kb.ring("pT", [128, 4, 128], BF16, 5)
        gkr = kb.ring("gk", [128, 256], BF16, 5)
        denr = kb.ring("den", [128, 8], F32, 4)
        sqh = kb.ring("sqh", [128, 256], F32, 2)
        ssr = kb.ring("ssr", [128, 8], F32, 2)
        mxr = kb.ring("mxr", [128, 256], F32, 2)
        sgr = kb.ring("sgr", [128, 256], F32, 2)
        mst = kb.ring("mst", [128, 2, 128], BF16, 3)
        psS_r = Ring([(self.ps[0], self.ps[1]), (self.ps[2], self.ps[3])])
        psU_r = Ring(self.ps[4:6])
        psD_r = Ring(self.ps[6:7])
        psT_r = Ring(self.ps[7:8])
        mqv = self.mqT.t.rearrange("(c p) t -> p c t", p=128)
        mkv = self.mkT.t.rearrange("(c p) t -> p c t", p=128)
        mixv = self.mixT.t[0:256, :].rearrange("(c p) t -> p c t", p=128)
        order = [list(range(NB)), [1, 0] + list(range(NB - 1, 1, -1))]
        step_of = [{blk: i for i, blk in enumerate(order[d])} for d in range(2)]

        def need_out(blk):
            return not (last and blk < 2)

        def load(d, blk):
            q, k, kt, v = qr.next(), kr.next(), ktr.next(), vr.next()
            c0 = blk * 128
            kb.dma(q[:], mqv[:, :, c0:c0 + 128], [self.mqT], [q], q)
            kb.dma(k[:], mkv[:, :, c0:c0 + 128], [self.mkT], [k], k)
            kb.dma(kt[:], self.mk_tm[:, blk, :], [self.mk_tm], [kt], kt)
            kb.dma(v[:], self.mv_tm[:, blk, :], [self.mv_tm], [v], v)
            o = None
            second = (step_of[1 - d][blk], 1 - d) < (step_of[d][blk], d)
            if second and need_out(blk):
                o = orr.next()
                kb.dma(o[:], self.mo_tm[:, blk, :], [self.mo_tm], [o], o)
            return q, k, kt, v, o, second

        stA = {}

        def compute_a(key, d, blk, q, k, kt, v, o, second):
            mask = self.cb[:, 1 + d, :]
            out_needed = need_out(blk)
            pT = None
            if out_needed:
                psS = psS_r.next()
                for h in range(4):
                    hp, pr = h % 2, h // 2
                    self.mm(psS[hp][:, pr * 128:(pr + 1) * 128], k[hp * 64:(hp + 1) * 64, pr, :], q[hp * 64:(hp + 1) * 64, pr, :],
                            True, True, [k, q], [psS[hp]], nowaw=pr > 0)
                pT = pTr.next()
                for h in range(4):
                    hp, pr = h % 2, h // 2
                    self.act(pT[:, h, :], psS[hp][:, pr * 128:(pr + 1) * 128], AF.Identity, [psS[hp], A_], [pT], nowaw=h > 0,
                             scale=A_[:, d, blk, h:h + 1])
                pTf = pT[:].rearrange("p h q -> p (h q)")
                self.tt("pool", pTf, pTf, self.mask4[:, d, :], ALU.mult, [pT, self.mask4], [pT])
            gk = gkr.next()
            self.tt("dve", gk[:].rearrange("p (h c) -> p h c", c=64), kt[:].rearrange("p (h c) -> p h c", c=64),
                    G_[:, d, blk, :].unsqueeze(2).to_broadcast([128, 4, 64]), ALU.mult, [kt, G_], [gk])
            stA[key] = (pT, gk)

        def compute(key, d, blk, q, k, kt, v, o, second):
            out_needed = need_out(blk)
            pT, gk = stA.pop(key)
            psU, psD = psU_r.next(), psD_r.next()
            if out_needed:
                for h in range(4):
                    hp, pr = h % 2, h // 2
                    self.mm(psU[:, h * 65:(h + 1) * 65], pT[:, h, :], v[:, h * 65:(h + 1) * 65], True, False, [pT, v], [psU], nowaw=h > 0)
                    self.mm(psU[:, h * 65:(h + 1) * 65], q[hp * 64:(hp + 1) * 64, pr, :],
                            Cbf[d][hp * 64:(hp + 1) * 64, pr, hp * 65:(hp + 1) * 65], False, True, [q, Cbf[d]], [psU], nowaw=True)
            for pr in range(2):
                self.mm(psD[:, pr * 130:(pr + 1) * 130], gk[:, pr * 128:(pr + 1) * 128], v[:, pr * 130:(pr + 1) * 130], True, True,
                        [gk, v], [psD], nowaw=pr > 0)
            for pr in range(2):
                self.stt("dve", Cst[d][:, pr, :], Cst[d][:, pr, :], DEC[:, d, blk, pr:pr + 1], psD[:, pr * 130:(pr + 1) * 130],
                         ALU.mult, ALU.add, [Cst[d], DEC, psD], [Cst[d]])
            self.cp("act", Cbf[d][:], Cst[d][:], [Cst[d]], [Cbf[d]])
            if not out_needed or M2S < 3:
                return
            den = denr.next()
            Uv = psU[:, 0:260].rearrange("p (h c) -> p h c", c=65)
            self.tt("dve", den[:, 0:4], Uv[:, :, 64], E_[:, d, blk, :], ALU.mult, [psU, E_], [den])
            self.stt("dve", den[:, 4:8], den[:, 0:4], -1.0, den[:, 0:4], ALU.mult, ALU.max, [den], [den])
            kb.op("dve", lambda e: e.tensor_scalar_max(out=den[:, 4:8], in0=den[:, 4:8], scalar1=1.0), [den], [den])
            kb.op("dve", lambda e: e.reciprocal(out=den[:, 4:8], in_=den[:, 4:8]), [den], [den])
            self.tt("dve", den[:, 0:4], E_[:, d, blk, :], den[:, 4:8], ALU.mult, [den, E_], [den])
            for h in range(4):
                hs = hsum[:, blk, h * 64:(h + 1) * 64]
                if not second:
                    self.act(hs, Uv[:, h, 0:64], AF.Identity, [psU, den], [hsb[blk]], nowaw=h > 0, scale=den[:, h:h + 1])
                else:
                    self.stt("dve", hs, Uv[:, h, 0:64], den[:, h:h + 1], hs, ALU.mult, ALU.add, [psU, den, hsb[blk]], [hsb[blk]])
            if not second or M2S < 4:
                return
            sq, ss = sqh.next(), ssr.next()
            hb = hsum[:, blk, :]
            self.tt("dve", sq[:], hb, hb, ALU.mult, [hsb[blk]], [sq])
            kb.op("dve", lambda e: e.reduce_sum(out=ss[:, 0:4], in_=sq[:].rearrange("p (h c) -> p h c", c=64), axis=AX.X), [sq], [ss])
            self.act(ss[:, 0:4], ss[:, 0:4], AF.Sqrt, [ss], [ss], scale=1.0 / 64.0, bias=self.epsb[:, 0:1])
            kb.op("dve", lambda e: e.reciprocal(out=ss[:, 4:8], in_=ss[:, 0:4]), [ss], [ss])
            mx, sg = mxr.next(), sgr.next()
            self.act(sg[:], o[:], AF.Sigmoid, [o], [sg])
            self.tt("dve", mx[:].rearrange("p (h c) -> p h c", c=64), hb.rearrange("p (h c) -> p h c", c=64),
                    ss[:, 4:8].unsqueeze(2).to_broadcast([128, 4, 64]), ALU.mult, [hsb[blk], ss], [mx])
            self.tt("pool", mx[:], mx[:], hg[:], ALU.mult, [mx, hg], [mx])
            self.tt("pool", mx[:], mx[:], sg[:], ALU.mult, [mx, sg], [mx])
            psT = psT_r.next()
            for c in range(2):
                kb.op("pe", lambda e: e.transpose(psT[:, c * 128:(c + 1) * 128], mx[:, c * 128:(c + 1) * 128], self.ident_f),
                      [mx, self.cf], [psT], nowaw=c > 0)
            st = mst.next()
            self.cp("act", st[:], psT[:, 0:256].rearrange("p (c t) -> p c t", c=2), [psT], [st])
            kb.dma(mixv[:, :, blk * 128:(blk + 1) * 128], st[:], [st], [self.mixT], st, nowaw=True)

        seq = []
        for i in range(NB):
            seq.append((0, order[0][i]))
            seq.append((1, order[1][i]))
        PF = 3
        LA = 2
        pend = {}
        for j in range(min(PF, len(seq))):
            pend[j] = load(*seq[j])
        for j in range(len(seq)):
            if j + PF < len(seq):
                pend[j + PF] = load(*seq[j + PF])
            compute_a(j, *seq[j], *pend[j])
            if j >= LA:
                compute(j - LA, *seq[j - LA], *pend.pop(j - LA))
        for j in range(max(len(seq) - LA, 0), len(seq)):
            compute(j, *seq[j], *pend.pop(j))
        kb.pop()

    def phase_attn(self, l, last):
        kb = self.kb
        NB, L = self.NB, self.L
        kb.push()
        aq = kb.sb("aq", [128, 4, L], BF16)
        ak = kb.sb("ak", [128, 2, L], BF16)
        av = kb.sb("av", [128, NB, 130], BF16)
        aqv = self.aqT.t.rearrange("(c p) t -> p c t", p=128)
        akv = self.akT.t.rearrange("(c p) t -> p c t", p=128)
        for c in range(2):
            kb.dma(ak[:, c, :], akv[:, c, :], [self.akT], [ak], ak, nowaw=c > 0)
        kb.dma(av[:], self.av_tm[:, :, :], [self.av_tm], [av], av)
        for c in range(4):
            kb.dma(aq[:, c, :], aqv[:, c, :], [self.aqT], [aq], aq, nowaw=c > 0)
        se = kb.sb("se", [128, 8], F32)
        srow = kb.sb("srow", [128, 8, 128], F32)
        kb.dma(se[64:65, :], self.din["sink"][l:l + 1, :], [self.din["sink"]], [se], se)
        self.act(se[64:65, :], se[64:65, :], AF.Exp, [se], [se])
        self.cp("dve", srow[64:65, :, :], se[64:65, :].unsqueeze(2).to_broadcast([1, 8, 128]), [se], [srow])
        pTr = kb.ring("apT", [128, 4, 128], BF16, 5)
        negm = kb.sb("negm", [128, 2, 2, 128], BF16)
        for i_ in range(2):
            for r_ in range(2):
                self.ts("dve", negm[:, i_, r_, :], self.cf[:, 1 + i_, :], -1.0, 30000.0, ALU.add, ALU.mult, [self.cf], [negm],
                        nowaw=(i_ + r_) > 0)
        rdr = kb.ring("rd", [128, 512], F32, 4)
        hir = kb.ring("hi", [128, 512], BF16, 3)
        lor = kb.ring("lo", [128, 512], BF16, 3)
        bcr = kb.ring("bc", [64, 512], F32, 2)
        ostr = kb.ring("ost", [64, 512], BF16, 3)
        psS_r = Ring([(self.ps[0], self.ps[1]), (self.ps[2], self.ps[3])])
        psO_r = Ring(self.ps[4:7])
        psB_r = Ring(self.ps[7:8])
        nxb = self.S // 128
        qblocks = ([] if last else [0, 1]) + list(range(2, NB))
        groups = []
        for qb in qblocks:
            if qb < 2:
                kbs = [(0, None), (1, None)]
            else:
                i = qb - 2
                kbs = [(0, None), (1, None)]
                if i >= 1:
                    kbs.append((qb - 1, 1))
                kbs.append((qb, None))
                if i < nxb - 1:
                    kbs.append((qb + 1, 0))
            for g in range(2):
                groups.append((qb, g, kbs))
        items = [(gi, n) for gi, grp in enumerate(groups) for n in range(len(grp[2]))]
        psO_of = {}
        pT_of = {}
        fin_state = {}

        def stage_a(it):
            gi, n = it
            qb, g, kbs = groups[gi]
            kbk, msk = kbs[n]
            psS = psS_r.next()
            for half in range(2):
                self.mm(psS[half][:, 0:256], ak[half * 64:(half + 1) * 64, g, kbk * 128:(kbk + 1) * 128],
                        aq[half * 64:(half + 1) * 64, 2 * g:2 * g + 2, qb * 128:(qb + 1) * 128], True, msk is None, [ak, aq], [psS[half]])
                if msk is not None:
                    self.mm(psS[half][:, 0:256], self.cb[:, 0, :], negm[:, msk, :, :], False, True,
                            [self.cb, negm], [psS[half]], nowaw=True)
            pT = pTr.next()
            pTv = pT[:].rearrange("p (jj h) q -> p jj h q", h=2)
            for half in range(2):
                self.act(pTv[:, :, half, :], psS[half][:, 0:256].rearrange("p (jj q) -> p jj q", jj=2), AF.Exp,
                         [psS[half]], [pT], nowaw=half > 0, scale=0.125)
            pT_of[it] = pT

        def stage_pv(it):
            gi, n = it
            qb, g, kbs = groups[gi]
            kbk, msk = kbs[n]
            if n == 0:
                psO_of[gi] = psO_r.next()
            psO = psO_of[gi]
            pT = pT_of.pop(it)
            pTf = pT[:].rearrange("p j q -> p (j q)")
            self.mm(psO[0:65, :], av[:, kbk, g * 65:(g + 1) * 65], pTf, n == 0, n == len(kbs) - 1, [av, pT], [psO], nowaw=n > 0)

        def fin_dve(gi):
            qb, g, kbs = groups[gi]
            psO = psO_of[gi]
            rd = rdr.next()
            self.tt("dve", rd[64:65, :], psO[64:65, :], srow[64:65, 4 * g:4 * g + 4, :].rearrange("p j q -> p (j q)"), ALU.add,
                    [psO, srow], [rd])
            kb.op("dve", lambda e: e.reciprocal(out=rd[64:65, :], in_=rd[64:65, :]), [rd], [rd])
            fin_state[gi] = rd

        def fin_pe(gi):
            qb, g, kbs = groups[gi]
            psO = psO_of.pop(gi)
            rd = fin_state.pop(gi)
            psB = psB_r.next()
            self.mm(psB[0:64, :], self.cf[64:65, 3, 0:64], rd[64:65, :], True, True, [self.cf, rd], [psB])
            bc = bcr.next()
            self.cp("act", bc[:], psB[0:64, :], [psB], [bc])
            ost = ostr.next()
            self.tt("dve", ost[:], psO[0:64, :], bc[:], ALU.mult, [psO, bc], [ost])
            r0 = 256 + 4 * g * 64
            dst = self.mixT.t[r0:r0 + 256, qb * 128:(qb + 1) * 128].rearrange("(j d) q -> d j q", d=64)
            kb.dma(dst, ost[:].rearrange("d (j q) -> d j q", j=4), [ost], [self.mixT], ost, nowaw=True)

        LA = 2
        pend = []
        finq = []

        def do_pv():
            it = pend.pop(0)
            stage_pv(it)
            gi, n = it
            ready = [g_ for (g_, age) in finq if age >= 1]
            finq[:] = [(g_, age + 1) for (g_, age) in finq if age < 1]
            if n == len(groups[gi][2]) - 1:
                fin_dve(gi)
                finq.append((gi, 0))
            for g_ in ready:
                fin_pe(g_)

        for it in items:
            stage_a(it)
            pend.append(it)
            if len(pend) > LA:
                do_pv()
        while pend:
            do_pv()
        for (g_, age) in finq:
            fin_pe(g_)
        kb.pop()

    def phase_conv(self, l, last):
        kb = self.kb
        kb.push()
        dww = kb.sb("dww", [128, 2, 31], F32)
        kb.dma(dww[:], self.din["dw_w"][l, :, :, :], [self.din["dw_w"]], [dww], dww)
        Dg = kb.sb("Dg", [128, 2, 31, 128], BF16)
        n = 0
        for ch in range(2):
            for k in range(31):
                eng = "pool" if n % 2 else "dve"
                kb.op(eng, lambda e: e.tensor_scalar_mul(out=Dg[:, ch, k, :], in0=self.cb[:, 0, :], scalar1=dww[:, ch, k:k + 1]),
                      [self.cb, dww], [Dg], nowaw=n > 0)
                n += 1
        pwf = kb.sb("pwf", [128, 2, 256], F32)
        pwb = kb.sb("pwb", [128, 2, 256], BF16)
        kb.dma(pwf[:], self.din["pw_w"][l, :, :, :], [self.din["pw_w"]], [pwf], pwf)
        self.cp("dve", pwb[:], pwf[:], [pwf], [pwb])
        dwb, lng, lnb = self.small["dw_b"], self.small["ln_g"], self.small["ln_b"]
        ytr = kb.ring("yt", [128, 2, 542], BF16, 3)
        yshr = kb.ring("ysh", [128, 2, 542], BF16, 3)
        zr = kb.ring("z", [128, 2, 512], F32, 3)
        zbr = kb.ring("zb", [128, 2, 512], BF16, 3)
        zqr = kb.ring("zq", [128, 2, 512], BF16, 3)
        mr = kb.ring("mean", [128, 512], F32, 2)
        vr = kb.ring("var", [128, 512], F32, 2)
        zcr = kb.ring("zc", [128, 512], F32, 2)
        sr = kb.ring("s", [128, 2, 512], BF16, 2)
        cst = kb.ring("cst", [128, 512], BF16, 3)
        psC_r = Ring(self.ps[0:4])
        psM_r = Ring(self.ps[4:5])
        psQ_r = Ring(self.ps[5:6])
        psP_r = Ring(self.ps[6:8])
        yv = self.yT.t.rearrange("(c p) t -> p c t", p=128)
        tiles = [t for t in self.tiles if not (last and t[2] == "c")]

        def load(ti):
            e0, w, kind = tiles[ti]
            lo_, hi_ = (0, CTX) if kind == "c" else (CTX, self.L)
            yt = ytr.next()
            a, b = max(e0 - 15, lo_), min(e0 + w + 15, hi_)
            if a > e0 - 15:
                kb.op("pool", lambda e: e.memset(yt[:, :, 0:15], 0.0), [], [yt])
            if b < e0 + w + 15:
                kb.op("pool", lambda e: e.memset(yt[:, :, w + 15:w + 30], 0.0), [], [yt], nowaw=a > e0 - 15)
            kb.dma(yt[:, :, a - (e0 - 15):b - (e0 - 15)], yv[:, :, a:b], [self.yT], [yt], yt, nowaw=(a > e0 - 15 or b < e0 + w + 15))
            ys = yshr.next()
            p0 = e0 - 15
            if a > p0:
                kb.op("pool", lambda e: e.memset(ys[:, :, 0:14], 0.0), [], [ys])
            if b < e0 + w + 15:
                kb.op("pool", lambda e: e.memset(ys[:, :, w + 14:w + 30], 0.0), [], [ys], nowaw=a > p0)
            a2 = max(a, p0 + 1)
            kb.dma(ys[:, :, a2 - p0 - 1:b - p0 - 1], yv[:, :, a2:b], [self.yT], [ys], ys, nowaw=(a > p0 or b < e0 + w + 15))
            return yt, ys

        def stage_a(ti, yts):
            e0, w, kind = tiles[ti]
            yt, ys = yts
            z, zb, zq = zr.next(), zbr.next(), zqr.next()
            for ch in range(2):
                psC = psC_r.next()
                for k in range(31):
                    src = yt[:, ch, k:k + w] if k % 2 == 0 else ys[:, ch, k - 1:k - 1 + w]
                    self.mm(psC[:, :w], Dg[:, ch, k, :], src, k == 0, k == 30, [Dg, yt, ys], [psC], nowaw=k > 0)
                self.act(z[:, ch, :w], psC[:, :w], AF.Identity, [psC, dwb], [z], nowaw=ch > 0, bias=dwb[:, l, ch:ch + 1], scale=1.0)
                self.act(zq[:, ch, :w], psC[:, :w], AF.Square, [psC, dwb], [zq], nowaw=ch > 0, bias=dwb[:, l, ch:ch + 1], scale=1.0)
                self.cp("dve", zb[:, ch, :w], z[:, ch, :w], [z], [zb], nowaw=ch > 0)
            return z, zb, zq

        def stage_b(ti, z, zb, zq):
            e0, w, kind = tiles[ti]
            psM, psQ = psM_r.next(), psQ_r.next()
            for ch in range(2):
                self.mm(psM[:, :w], self.ones256[:], zb[:, ch, :w], ch == 0, ch == 1, [self.ones256, zb], [psM], nowaw=ch > 0)
            for ch in range(2):
                self.mm(psQ[:, :w], self.ones256[:], zq[:, ch, :w], ch == 0, ch == 1, [self.ones256, zq], [psQ], nowaw=ch > 0)
            mean, var = mr.next(), vr.next()
            self.cp("act", mean[:, :w], psM[:, :w], [psM], [mean])
            self.act(var[:, :w], psM[:, :w], AF.Square, [psM], [var])
            self.tt("dve", var[:, :w], psQ[:, :w], var[:, :w], ALU.subtract, [psQ, var], [var])
            self.act(var[:, :w], var[:, :w], AF.Sqrt, [var], [var], bias=self.epsb[:, 0:1], scale=1.0)
            kb.op("dve", lambda e: e.reciprocal(out=var[:, :w], in_=var[:, :w]), [var], [var])
            s = sr.next()
            for ch in range(2):
                zc = zcr.next()
                self.tt("pool", zc[:, :w], z[:, ch, :w], mean[:, :w], ALU.subtract, [z, mean], [zc])
                self.tt("dve", zc[:, :w], zc[:, :w], var[:, :w], ALU.mult, [zc, var], [zc])
                self.act(s[:, ch, :w], zc[:, :w], AF.Silu, [zc, lng, lnb], [s], nowaw=ch > 0,
                         bias=lnb[:, l, ch:ch + 1], scale=lng[:, l, ch:ch + 1])
            for oc in range(2):
                psP = psP_r.next()
                for ic in range(2):
                    self.mm(psP[:, :w], pwb[:, ic, oc * 128:(oc + 1) * 128], s[:, ic, :w], ic == 0, ic == 1, [pwb, s], [psP], nowaw=ic > 0)
                st = cst.next()
                self.cp("act", st[:, :w], psP[:, :w], [psP], [st])
                kb.dma(self.mixT.t[768 + oc * 128:768 + (oc + 1) * 128, e0:e0 + w], st[:, :w], [st], [self.mixT], st, nowaw=True)

        pend = {0: load(0)}
        prevz = None
        for ti in range(len(tiles)):
            if ti + 1 < len(tiles):
                pend[ti + 1] = load(ti + 1)
            zs = stage_a(ti, pend.pop(ti))
            if prevz is not None:
                stage_b(ti - 1, *prevz)
            prevz = zs
        stage_b(len(tiles) - 1, *prevz)
        kb.pop()

    def phase_p3(self, l, xsrc, xdst, last):
        kb = self.kb
        kb.push()
        wo = kb.sb("wo", [128, 8, D], BF16)
        wsrc = self.din["w_out"]
        self.load_cast(wo, lambda c0, cw: wsrc[l, :, :, c0:c0 + cw], wsrc, D, 8)
        xr = kb.ring("xt", [128, 8, 512], F32, 2)
        mr = kb.ring("mt", [128, 8, 512], BF16, 2)
        orr = kb.ring("xo", [128, 8, 512], F32, 2)
        psr = Ring(self.ps)
        xv = xsrc.t.rearrange("(k p) t -> p k t", p=128)
        ov = xdst.t.rearrange("(k p) t -> p k t", p=128)
        mv = self.mixT.t.rearrange("(k p) t -> p k t", p=128)
        tiles = [t for t in self.tiles if not (last and t[2] == "c")]

        def load(ti):
            e0, w, kind = tiles[ti]
            xt, mt = xr.next(), mr.next()
            kb.dma(mt[:, :, :w], mv[:, :, e0:e0 + w], [self.mixT], [mt], mt)
            kb.dma(xt[:, :, :w], xv[:, :, e0:e0 + w], [xsrc], [xt], xt)
            return xt, mt

        pend = {0: load(0)}
        for ti in range(len(tiles)):
            if ti + 1 < len(tiles):
                pend[ti + 1] = load(ti + 1)
            e0, w, kind = tiles[ti]
            v = 0 if kind == "x" else 1
            xt, mt = pend.pop(ti)
            xo = orr.next()
            for m in range(8):
                ps = psr.next()
                for k in range(8):
                    self.mm(ps[:, :w], wo[:, k, m * 128:(m + 1) * 128], mt[:, k, :w], k == 0, k == 7, [wo, mt], [ps], nowaw=k > 0)
                self.stt("dve", xo[:, m, :w], ps[:, :w], self.modcol(l, v, 2, m), xt[:, m, :w], ALU.mult, ALU.add,
                         [ps, self.modv, xt], [xo], nowaw=m > 0)
            kb.dma(ov[:, :, e0:e0 + w], xo[:, :, :w], [xo], [xdst], xo, nowaw=True)
        kb.pop()

    def phase_f1(self, l, xsrc, last):
        kb = self.kb
        kb.push()
        wf = kb.sb("wfi", [128, 8, 2 * HID], BF16)
        wsrc = self.din["w_fi"]
        self.load_cast(wf, lambda c0, cw: wsrc[l, :, :, c0:c0 + cw], wsrc, 2 * HID, 8)
        xr = kb.ring("xt", [128, 8, 512], F32, 2)
        hr = kb.ring("hT", [128, 8, 512], BF16, 2)
        sqr = kb.ring("sq", [128, 512], BF16, 2)
        tmpr = kb.ring("tmp", [128, 512], F32, 3)
        rstd_r = kb.ring("rstd", [128, 512], F32, 2)
        sgr = kb.ring("sg", [128, 512], F32, 3)
        ast = kb.ring("ast", [128, 512], BF16, 6)
        psr = Ring(self.ps[0:6])
        psn = Ring(self.ps[6:8])
        xv = xsrc.t.rearrange("(k p) t -> p k t", p=128)
        tiles = [t for t in self.tiles if not (last and t[2] == "c")]

        def load(ti):
            e0, w, kind = tiles[ti]
            xt = xr.next()
            kb.dma(xt[:, :, :w], xv[:, :, e0:e0 + w], [xsrc], [xt], xt)
            return xt

        pend = {0: load(0)}
        for ti in range(len(tiles)):
            if ti + 1 < len(tiles):
                pend[ti + 1] = load(ti + 1)
            e0, w, kind = tiles[ti]
            v = 0 if kind == "x" else 1
            xt = pend.pop(ti)
            hT = hr.next()
            self.norm_tile(xt, w, lambda k: self.gm[:, l, v, 1, k:k + 1], lambda k: self.modcol(l, v, 3, k),
                           hT, sqr, tmpr, rstd_r.next(), psn.next())
            for j in range(HID // 128):
                psg, psu = psr.next(), psr.next()
                for k in range(8):
                    self.mm(psg[:, :w], wf[:, k, j * 128:(j + 1) * 128], hT[:, k, :w], k == 0, k == 7, [wf, hT], [psg], nowaw=k > 0)
                for k in range(8):
                    self.mm(psu[:, :w], wf[:, k, HID + j * 128:HID + (j + 1) * 128], hT[:, k, :w], k == 0, k == 7, [wf, hT], [psu], nowaw=k > 0)
                sg = sgr.next()
                self.act(sg[:, :w], psg[:, :w], AF.Silu, [psg], [sg])
                st = ast.next()
                self.tt("dve", st[:, :w], psu[:, :w], sg[:, :w], ALU.mult, [psu, sg], [st])
                kb.dma(self.aT.t[j * 128:(j + 1) * 128, e0:e0 + w], st[:, :w], [st], [self.aT], st, nowaw=True)
        kb.pop()

    def phase_f2(self, l, xsrc, xdst, last):
        kb = self.kb
        kb.push()
        NK = HID // 128
        wf = kb.sb("wfo", [128, NK, D], BF16)
        wsrc = self.din["w_fo"]
        self.load_cast(wf, lambda c0, cw: wsrc[l, :, :, c0:c0 + cw], wsrc, D, NK)
        xr = kb.ring("xt", [128, 8, 512], F32, 2)
        ar = kb.ring("at", [128, NK, 512], BF16, 2)
        orr = kb.ring("xo", [128, 8, 512], F32, 2)
        psr = Ring(self.ps)
        xv = xsrc.t.rearrange("(k p) t -> p k t", p=128)
        ov = xdst.t.rearrange("(k p) t -> p k t", p=128)
        av = self.aT.t.rearrange("(k p) t -> p k t", p=128)
        tiles = [t for t in self.tiles if not (last and t[2] == "c")]

        def load(ti):
            e0, w, kind = tiles[ti]
            xt, at = xr.next(), ar.next()
            kb.dma(at[:, 0:11, :w], av[:, 0:11, e0:e0 + w], [self.aT], [at], at)
            kb.dma(at[:, 11:NK, :w], av[:, 11:NK, e0:e0 + w], [self.aT], [at], at, nowaw=True)
            kb.dma(xt[:, :, :w], xv[:, :, e0:e0 + w], [xsrc], [xt], xt)
            return xt, at

        pend = {0: load(0)}
        for ti in range(len(tiles)):
            if ti + 1 < len(tiles):
                pend[ti + 1] = load(ti + 1)
            e0, w, kind = tiles[ti]
            v = 0 if kind == "x" else 1
            xt, at = pend.pop(ti)
            xo = orr.next()
            for m in range(8):
                ps = psr.next()
                for k in range(NK):
                    self.mm(ps[:, :w], wf[:, k, m * 128:(m + 1) * 128], at[:, k, :w], k == 0, k == NK - 1, [wf, at], [ps], nowaw=k > 0)
                self.stt("dve", xo[:, m, :w], ps[:, :w], self.modcol(l, v, 5, m), xt[:, m, :w], ALU.mult, ALU.add,
                         [ps, self.modv, xt], [xo], nowaw=m > 0)
            kb.dma(ov[:, :, e0:e0 + w], xo[:, :, :w], [xo], [xdst], xo, nowaw=True)
        kb.pop()

    def phase_final(self, xsrc):
        kb = self.kb
        kb.push()
        xr = kb.ring("xt", [128, 8, 512], F32, 2)
        yr = kb.ring("yT", [128, 8, 512], F32, 2)
        sqr = kb.ring("sq", [128, 512], BF16, 2)
        rstd_r = kb.ring("rstd", [128, 512], F32, 2)
        outr = kb.ring("yo", [128, D], F32, 3)
        psr = Ring(self.ps[0:6])
        psn = Ring(self.ps[6:8])
        xv = xsrc.t.rearrange("(k p) t -> p k t", p=128)
        fg = self.small["fgain"]
        tiles = [t for t in self.tiles if t[2] == "x"]

        def load(ti):
            e0, w, kind = tiles[ti]
            xt = xr.next()
            kb.dma(xt[:, :, :w], xv[:, :, e0:e0 + w], [xsrc], [xt], xt)
            return xt

        pend = {0: load(0)}
        for ti in range(len(tiles)):
            if ti + 1 < len(tiles):
                pend[ti + 1] = load(ti + 1)
            e0, w, kind = tiles[ti]
            xt = pend.pop(ti)
            yT = yr.next()
            self.norm_tile(xt, w, lambda k: fg[:, k:k + 1], None, yT, sqr, None, rstd_r.next(), psn.next(), plain=True)
            for j in range(w // 128):
                yo = outr.next()
                for half in range(2):
                    ps = psr.next()
                    for kk in range(4):
                        k = half * 4 + kk
                        kb.op("pe", lambda e: e.transpose(ps[:, kk * 128:(kk + 1) * 128], yT[:, k, j * 128:(j + 1) * 128], self.ident_f),
                              [yT, self.cf], [ps], nowaw=kk > 0)
                    self.cp("act" if half == 0 else "dve", yo[:, half * 512:(half + 1) * 512], ps[:, :], [ps], [yo], nowaw=half > 0)
                t0 = e0 - CTX + j * 128
                kb.dma(self.y[t0:t0 + 128, :], yo[:], [yo], [self.y], yo, nowaw=True)
        kb.pop()


_CACHE = {}


def kernel(**inputs):
    S = inputs["x"].shape[1]
    B = inputs["x"].shape[0]
    if S not in _CACHE:
        _CACHE[S] = Prog(S).build()
    nc = _CACHE[S]
    sh = prep_shared(inputs, S)
    in_maps = []
    for b in range(B):
        m = dict(sh)
        m.update(prep_core(inputs, b, S))
        in_maps.append(m)
    res = run_bass_kernel_spmd(nc, in_maps, core_ids=list(range(B)))
    return np.stack([np.asarray(r["y"], dtype=np.float32) for r in res.results], axis=0)
```

```python
import contextlib
import numpy as np
import concourse.bass as bass
import concourse.mybir as mybir
from concourse.bass_utils import run_bass_kernel_spmd

F32 = mybir.dt.float32
BF16 = mybir.dt.bfloat16
AF = mybir.ActivationFunctionType
ALU = mybir.AluOpType
AX = mybir.AxisListType

D = 1024
CTX = 256
DEPTH = 2
HID = 2816
EPS = 1e-6
NFM = 20
TM0 = NFM * 128
NTM = 912
NC_EXT = TM0 + NTM


class Buf:
    def __init__(self, name, t):
        self.name = name
        self.t = t
        self.writers = []
        self.readers = []
        self.epoch_deps = []
        self.sem = None
        self.excl = False

    def __getitem__(self, k):
        return self.t[k]


class Sem:
    def __init__(self, h, key):
        self.h = h
        self.key = key
        self.cnt = 0


class KB:
    def __init__(self, nc, es):
        self.nc = nc
        self.es = es
        self.eng = {"pe": nc.tensor, "act": nc.scalar, "dve": nc.vector, "pool": nc.gpsimd, "sp": nc.sync}
        self.esem = {}
        for e in ("pe", "act", "dve", "pool"):
            self.esem[e] = Sem(es.enter_context(nc.semaphore("se_" + e)), "e_" + e)
        self.dpool = [Sem(es.enter_context(nc.semaphore("sd%d" % i)), "d%d" % i) for i in range(72)]
        self.dused = []
        self.waited = {e: {} for e in self.eng}
        self.uid = 0
        self.stacks = [es]
        self.outstanding = {}
        self.phase_bufs = [[]]

    def push(self):
        s = contextlib.ExitStack()
        s.__enter__()
        self.stacks.append(s)
        self.phase_bufs.append([])

    def pop(self):
        self.barrier()
        for b in self.phase_bufs.pop():
            if b.sem is not None:
                self.dpool.append(b.sem)
                b.sem = None
        s = self.stacks.pop()
        s.__exit__(None, None, None)

    def sb(self, name, shape, dt):
        self.uid += 1
        h = self.stacks[-1].enter_context(self.nc.sbuf_tensor("%s_%d" % (name, self.uid), list(shape), dt))
        b = Buf(name, h)
        self.phase_bufs[-1].append(b)
        return b

    def ring(self, name, shape, dt, n):
        return Ring([self.sb("%s%d" % (name, i), shape, dt) for i in range(n)])

    def dram(self, name, shape, dt, kind="Internal"):
        t = self.nc.dram_tensor(name, list(shape), dt, kind=kind)
        return Buf(name, t.ap())

    def _deps(self, reads, writes, nowaw, own=None):
        deps = {}

        def add(evs):
            for (s, v) in evs:
                if s.key not in deps or deps[s.key][1] < v:
                    deps[s.key] = (s, v)

        for b in reads:
            add(b.writers)
            if b.excl:
                add([ev for ev in b.readers if ev[0].key != own])
        for b in writes:
            if nowaw and not b.readers and b.writers:
                add(b.epoch_deps)
            else:
                add(b.readers)
                add(b.writers)
        return deps

    def _emit_waits(self, engine, deps):
        e = self.eng[engine]
        w = self.waited[engine]
        for key, (s, v) in deps.items():
            if engine == "pe" and key == "e_pe":
                continue
            if w.get(key, 0) >= v:
                continue
            e.wait_ge(s.h, v)
            w[key] = v

    def _commit(self, ev, reads, writes, nowaw):
        for b in reads:
            b.readers.append(ev)
        for b in writes:
            if nowaw and not b.readers and b.writers:
                b.writers.append(ev)
            else:
                b.epoch_deps = b.readers + b.writers
                b.writers = [ev]
                b.readers = []
        self.outstanding[ev[0].key] = ev

    def op(self, engine, fn, reads=(), writes=(), nowaw=False):
        deps = self._deps(reads, writes, nowaw, own="e_" + engine)
        self._emit_waits(engine, deps)
        inst = fn(self.eng[engine])
        s = self.esem[engine]
        s.cnt += 1
        inst.then_inc(s.h, 1)
        self._commit((s, s.cnt), reads, writes, nowaw)
        return inst

    def dma(self, out, in_, reads, writes, sembuf, q="sp", nowaw=False):
        deps = self._deps(reads, writes, nowaw)
        self._emit_waits(q, deps)
        if sembuf.sem is None:
            sembuf.sem = self.dpool.pop()
        s = sembuf.sem
        inst = self.eng[q].dma_start(out=out, in_=in_)
        s.cnt += 16
        inst.then_inc(s.h, 16)
        self._commit((s, s.cnt), reads, writes, nowaw)
        return inst

    def barrier(self):
        evs = dict(self.outstanding)
        for engine in self.eng:
            self._emit_waits(engine, evs)

    def final_wait(self):
        self._emit_waits("sp", dict(self.outstanding))


class Ring:
    def __init__(self, bufs):
        self.bufs = bufs
        self.i = 0

    def next(self):
        b = self.bufs[self.i % len(self.bufs)]
        self.i += 1
        return b


def _swap(dd):
    return dd + 16 if (dd % 32) < 16 else dd - 16


def ext_cols():
    cols = []
    cols += list(range(0, 256))
    cols += list(range(256, 512))
    cols += [1040 + i for i in range(512)]
    cols += [1040 + (i // 64) * 64 + _swap(i % 64) for i in range(512)]
    for g in range(2):
        cols += [1552 + g * 64 + dd for dd in range(64)] * 2
    for g in range(2):
        cols += [1552 + g * 64 + _swap(dd) for dd in range(64)] * 2
    cols += list(range(1808, 2064))
    cols += list(range(2064, 2320))
    assert len(cols) == TM0
    cols += list(range(256, 512)) + list(range(512, 768)) + list(range(768, 1024))
    cols += list(range(1024, 1040)) + list(range(1680, 1808))
    assert len(cols) == NC_EXT
    return np.array(cols)


def pmaj(v, nk):
    sh = v.shape[:-1]
    a = v.reshape(sh + (nk, 128))
    return np.ascontiguousarray(np.moveaxis(a, -1, 0))


def rope_tables(S):
    half = 16
    freqs = (np.float32(10000.0) ** (-np.arange(half, dtype=np.float32) / np.float32(half))).astype(np.float32)
    t = np.arange(S)
    rows = (t // 64).astype(np.float32)
    colsp = (t % 64).astype(np.float32)
    cos_t = np.zeros((128, S), np.float32)
    sin_t = np.zeros((128, S), np.float32)
    for p in range(128):
        dd = p % 64
        pos = rows if dd < 32 else colsp
        d2 = dd % 32
        i = d2 % 16
        ang = (pos * freqs[i]).astype(np.float32)
        cos_t[p] = np.cos(ang)
        sgn = -1.0 if d2 < 16 else 1.0
        sin_t[p] = sgn * np.sin(ang)
    return cos_t, sin_t


def prep_shared(inp, S):
    f = lambda a: np.ascontiguousarray(np.asarray(a, dtype=np.float32))
    sh = {}
    cols = ext_cols()
    w_in = f(inp["w_in"])
    sh["w_ext"] = np.ascontiguousarray(w_in[:, :, cols].reshape(DEPTH, 8, 128, NC_EXT).transpose(0, 2, 1, 3))
    sh["w_mod"] = np.ascontiguousarray(f(inp["w_mod"]).reshape(DEPTH, 8, 128, 6 * D).transpose(0, 2, 1, 3))
    sh["b_mod"] = pmaj(f(inp["b_mod"]), 48)
    sh["ngain"] = pmaj(f(inp["norm_gain"]), 8)
    sh["fgain"] = pmaj(f(inp["final_gain"]), 8)
    sh["gbias"] = f(inp["mlstm_gate_bias"]).reshape(DEPTH, 16)
    sh["hgain"] = f(inp["mlstm_head_gain"])
    sh["sink"] = f(inp["attn_sink"])
    sh["dw_w"] = np.ascontiguousarray(f(inp["conv_dw_w"]).reshape(DEPTH, 31, 2, 128).transpose(0, 3, 2, 1))
    sh["dw_b"] = pmaj(f(inp["conv_dw_b"]), 2)
    sh["ln_g"] = pmaj(f(inp["conv_ln_g"]), 2)
    sh["ln_b"] = pmaj(f(inp["conv_ln_b"]), 2)
    sh["pw_w"] = np.ascontiguousarray(f(inp["conv_pw_w"]).reshape(DEPTH, 2, 128, 256).transpose(0, 2, 1, 3))
    sh["w_out"] = np.ascontiguousarray(f(inp["w_out"]).reshape(DEPTH, 8, 128, D).transpose(0, 2, 1, 3))
    sh["w_fi"] = np.ascontiguousarray(f(inp["w_ffn_in"]).reshape(DEPTH, 8, 128, 2 * HID).transpose(0, 2, 1, 3))
    sh["w_fo"] = np.ascontiguousarray(f(inp["w_ffn_out"]).reshape(DEPTH, 22, 128, D).transpose(0, 2, 1, 3))
    ident = np.eye(128, dtype=np.float32)
    s_i = np.arange(128)[:, None]
    t_i = np.arange(128)[None, :]
    consts = np.zeros((128, 4, 128), np.float32)
    consts[:, 0] = ident
    consts[:, 1] = (s_i <= t_i)
    consts[:, 2] = (s_i >= t_i)
    consts[:, 3] = 1.0
    sh["consts"] = consts
    cos_t, sin_t = rope_tables(S)
    sh["cos_t"] = cos_t
    sh["sin_t"] = sin_t
    return sh


def prep_core(inp, b, S):
    f = lambda a: np.ascontiguousarray(np.asarray(a, dtype=np.float32))
    pc = {}
    pc["x"] = f(inp["x"][b, :S])
    pc["ctx"] = f(inp["ctx"][b])
    cc = np.stack([f(inp["c"][b]), f(inp["c_ctx"])], axis=0)
    pc["cc"] = np.ascontiguousarray(cc.reshape(2, 8, 128).transpose(2, 1, 0))
    return pc


INPUT_SHAPES = lambda S: {
    "x": [S, D], "ctx": [CTX, D], "cc": [128, 8, 2],
    "w_ext": [DEPTH, 128, 8, NC_EXT], "w_mod": [DEPTH, 128, 8, 6 * D], "b_mod": [128, DEPTH, 48],
    "ngain": [128, DEPTH, 2, 8], "fgain": [128, 8], "gbias": [DEPTH, 16], "hgain": [DEPTH, 256],
    "sink": [DEPTH, 8], "dw_w": [DEPTH, 128, 2, 31], "dw_b": [128, DEPTH, 2], "ln_g": [128, DEPTH, 2],
    "ln_b": [128, DEPTH, 2], "pw_w": [DEPTH, 128, 2, 256], "w_out": [DEPTH, 128, 8, D],
    "w_fi": [DEPTH, 128, 8, 2 * HID], "w_fo": [DEPTH, 128, 22, D], "consts": [128, 4, 128],
    "cos_t": [128, S], "sin_t": [128, S],
}


class Prog:
    def __init__(self, S, dbg=(), upto=None, nlayers=DEPTH):
        self.S = S
        self.L = CTX + S
        self.NB = self.L // 128
        self.dbg = set(dbg)
        self.upto = upto
        self.nlayers = nlayers
        self.tiles = [(0, CTX, "c")] + [(CTX + 512 * i, 512, "x") for i in range(S // 512)]

    def mm(self, out, lhsT, rhs, start, stop, reads, writes, nowaw=False):
        return self.kb.op("pe", lambda e: e.matmul(out, lhsT, rhs, start=start, stop=stop), reads, writes, nowaw)

    def act(self, out, in_, func, reads, writes, nowaw=False, **kw):
        return self.kb.op("act", lambda e: e.activation(out=out, in_=in_, func=func, **kw), reads, writes, nowaw)

    def cp(self, eng, out, in_, reads, writes, nowaw=False):
        if eng == "act":
            return self.kb.op("act", lambda e: e.copy(out=out, in_=in_), reads, writes, nowaw)
        return self.kb.op(eng, lambda e: e.tensor_copy(out=out, in_=in_), reads, writes, nowaw)

    def tt(self, eng, out, in0, in1, op, reads, writes, nowaw=False):
        return self.kb.op(eng, lambda e: e.tensor_tensor(out=out, in0=in0, in1=in1, op=op), reads, writes, nowaw)

    def stt(self, eng, out, in0, scalar, in1, op0, op1, reads, writes, nowaw=False):
        return self.kb.op(eng, lambda e: e.scalar_tensor_tensor(out=out, in0=in0, scalar=scalar, in1=in1, op0=op0, op1=op1),
                          reads, writes, nowaw)

    def ts(self, eng, out, in0, s1, s2, op0, op1, reads, writes, nowaw=False):
        return self.kb.op(eng, lambda e: e.tensor_scalar(out=out, in0=in0, scalar1=s1, scalar2=s2, op0=op0, op1=op1),
                          reads, writes, nowaw)

    def dram(self, name, shape, dt):
        kind = "ExternalOutput" if name in self.dbg else "Internal"
        return self.kb.dram(name, shape, dt, kind=kind)

    def build(self):
        nc = bass.Bass("TRN2", target_bir_lowering=False)
        self.nc = nc
        S, L, NB = self.S, self.L, self.NB
        es = contextlib.ExitStack()
        with es:
            kb = KB(nc, es)
            self.kb = kb
            self.din = {k: kb.dram(k, shp, F32, kind="ExternalInput") for k, shp in INPUT_SHAPES(S).items()}
            self.y = kb.dram("y", [S, D], F32, kind="ExternalOutput")
            self.xT = [self.dram("xTa", [D, L], F32), self.dram("xTb", [D, L], F32)]
            self.mqT = self.dram("mqT", [256, L], BF16)
            self.mkT = self.dram("mkT", [256, L], BF16)
            self.mk_tm = self.dram("mk_tm", [128, NB, 256], BF16)
            self.mv_tm = self.dram("mv_tm", [128, NB, 260], BF16)
            self.mo_tm = self.dram("mo_tm", [128, NB, 256], F32)
            self.aqT = self.dram("aqT", [512, L], BF16)
            self.akT = self.dram("akT", [256, L], BF16)
            self.av_tm = self.dram("av_tm", [128, NB, 130], BF16)
            self.yT = self.dram("yT", [256, L], BF16)
            self.mixT = self.dram("mixT", [D, L], BF16)
            self.aT = self.dram("aT", [HID, L], BF16)
            if "hT" in self.dbg:
                self.dbg_hT = self.dram("hT", [D, L], BF16)
            self.ps = []
            for i in range(8):
                h = es.enter_context(nc.psum_tensor("psb%d" % i, [128, 512], F32))
                self.ps.append(Buf("ps%d" % i, h))
                self.ps[-1].excl = True
            self.cf = kb.sb("cf", [128, 4, 128], F32)
            self.cb = kb.sb("cb", [128, 4, 128], BF16)
            self.ones256 = kb.sb("o256", [128, 128], BF16)
            self.mask4 = kb.sb("mask4", [128, 2, 512], BF16)
            self.cc = kb.sb("cc", [128, 8, 2], F32)
            self.small = {}
            for nm in ("b_mod", "ngain", "fgain", "dw_b", "ln_g", "ln_b"):
                shp = INPUT_SHAPES(S)[nm]
                self.small[nm] = kb.sb(nm, shp, F32)
            self.modv = kb.sb("modv", [128, DEPTH, 2, 48], F32)
            self.gm = kb.sb("gm", [128, DEPTH, 2, 2, 8], F32)
            self.gates = kb.sb("gates", [128, NB, 16], F32)
            self.epsb = kb.sb("epsb", [128, 2], F32)
            self.init_consts()
            self.run()
            kb.final_wait()
        return nc

    def init_consts(self):
        kb = self.kb
        kb.dma(self.cf[:], self.din["consts"][:, :, :], [self.din["consts"]], [self.cf], self.cf)
        kb.dma(self.cc[:], self.din["cc"][:, :, :], [self.din["cc"]], [self.cc], self.cc)
        for nm, b in self.small.items():
            src = self.din[nm]
            idx = tuple(slice(None) for _ in INPUT_SHAPES(self.S)[nm])
            kb.dma(b[idx], src[idx], [src], [b], b)
        self.cp("dve", self.cb[:], self.cf[:], [self.cf], [self.cb])
        self.kb.op("dve", lambda e: e.tensor_scalar_mul(out=self.ones256[:], in0=self.cf[:, 3, :], scalar1=1.0 / 256.0),
                   [self.cf], [self.ones256])
        self.kb.op("dve", lambda e: e.memset(self.epsb[:], EPS), [], [self.epsb])
        for i in range(4):
            self.cp("dve", self.mask4[:, 0, i * 128:(i + 1) * 128], self.cf[:, 1, :], [self.cf], [self.mask4], nowaw=i > 0)
            self.cp("dve", self.mask4[:, 1, i * 128:(i + 1) * 128], self.cf[:, 2, :], [self.cf], [self.mask4], nowaw=True)

    @property
    def ident_f(self):
        return self.cf[:, 0, :]

    @property
    def ones_b(self):
        return self.cb[:, 3, :]

    def stop(self, name):
        return self.upto == name

    def run(self):
        self.phase_p0()
        if self.stop("p0"):
            return
        cur = 0
        for l in range(self.nlayers):
            last = l == DEPTH - 1
            self.phase_mod(l)
            if self.stop("mod"):
                return
            self.phase_p1(l, self.xT[cur], last)
            if self.stop("p1"):
                return
            self.phase_mlstm(l, last)
            if self.stop("mlstm"):
                return
            self.phase_attn(l, last)
            if self.stop("attn"):
                return
            self.phase_conv(l, last)
            if self.stop("conv"):
                return
            self.phase_p3(l, self.xT[cur], self.xT[1 - cur], last)
            cur = 1 - cur
            if self.stop("p3"):
                return
            self.phase_f1(l, self.xT[cur], last)
            if self.stop("f1"):
                return
            self.phase_f2(l, self.xT[cur], self.xT[1 - cur], last)
            cur = 1 - cur
            if self.stop("f2"):
                return
        self.phase_final(self.xT[cur])

    def phase_p0(self):
        kb = self.kb
        kb.push()
        xin = kb.ring("xin", [128, D], F32, 3)
        xst = kb.ring("xst", [128, 8, 128], F32, 3)
        xTv = self.xT[0].t.rearrange("(k p) t -> p k t", p=128)
        psr = Ring(self.ps)

        def load(blk):
            t = xin.next()
            if blk < 2:
                src, sb_ = self.din["ctx"][blk * 128:(blk + 1) * 128, :], self.din["ctx"]
            else:
                src, sb_ = self.din["x"][(blk - 2) * 128:(blk - 1) * 128, :], self.din["x"]
            kb.dma(t[:], src, [sb_], [t], t)
            return t

        pend = {0: load(0)}
        if self.NB > 1:
            pend[1] = load(1)
        for blk in range(self.NB):
            if blk + 2 < self.NB:
                pend[blk + 2] = load(blk + 2)
            t = pend.pop(blk)
            o = xst.next()
            for half in range(2):
                ps = psr.next()
                for kk in range(4):
                    k = half * 4 + kk
                    kb.op("pe", lambda e: e.transpose(ps[:, kk * 128:(kk + 1) * 128], t[:, k * 128:(k + 1) * 128], self.ident_f),
                          [t, self.cf], [ps], nowaw=kk > 0)
                self.cp("act" if half == 0 else "dve", o[:, half * 4:(half + 1) * 4, :],
                        ps[:, :].rearrange("p (k t) -> p k t", k=4), [ps], [o], nowaw=half > 0)
            kb.dma(xTv[:, :, blk * 128:(blk + 1) * 128], o[:], [o], [self.xT[0]], o, nowaw=True)
        kb.pop()

    def phase_mod(self, l):
        kb = self.kb
        kb.push()
        wst = kb.ring("wm", [128, 8, 512], F32, 2)
        sc = kb.sb("silu_c", [128, 8, 2], F32)
        self.act(sc[:], self.cc[:], AF.Silu, [self.cc], [sc])
        ps = self.ps[0]
        wm = self.din["w_mod"]
        first = True
        for piece in range(12):
            t = wst.next()
            kb.dma(t[:], wm[l, :, :, piece * 512:(piece + 1) * 512], [wm], [t], t)
            for cq in range(4):
                j = piece * 4 + cq
                for k in range(8):
                    self.mm(ps[:, 2 * j:2 * j + 2], t[:, k, cq * 128:(cq + 1) * 128], sc[:, k, :], k == 0, k == 7,
                            [t, sc], [ps], nowaw=not first)
                    first = False
        psv = ps[:, 0:96].rearrange("p (j c) -> p j c", c=2)
        bm = self.small["b_mod"]
        for v in range(2):
            self.tt("dve", self.modv[:, l, v, :], psv[:, :, v], bm[:, l, :], ALU.add, [ps, bm], [self.modv], nowaw=True)
        ng = self.small["ngain"]
        for v in range(2):
            for i in range(2):
                scv = self.modv[:, l, v, (3 * i + 1) * 8:(3 * i + 2) * 8]
                self.stt("dve", self.gm[:, l, v, i, :], scv, 1.0, ng[:, l, i, :], ALU.add, ALU.mult,
                         [self.modv, ng], [self.gm], nowaw=True)
        kb.pop()

    def modcol(self, l, v, m, k):
        return self.modv[:, l, v, m * 8 + k:m * 8 + k + 1]

    def load_cast(self, dst, src_ap_fn, src_buf, ncols, nk, engines=("dve", "pool", "act")):
        kb = self.kb
        kb.push()
        st = kb.ring("wst", [128, nk, 512], F32, 2)
        i = 0
        for c0 in range(0, ncols, 512):
            cw = min(512, ncols - c0)
            t = st.next()
            kb.dma(t[:, :, :cw], src_ap_fn(c0, cw), [src_buf], [t], t)
            eng = engines[i % len(engines)]
            self.cp(eng, dst[:, :, c0:c0 + cw], t[:, :, :cw], [t], [dst], nowaw=i > 0)
            i += 1
        kb.pop()

    def norm_tile(self, xt, w, gm_fn, sh_fn, hT, sqr, tmpr, rstd, psb, plain=False):
        for k in range(8):
            sq = sqr.next()
            self.act(sq[:, :w], xt[:, k, :w], AF.Square, [xt], [sq])
            self.mm(psb[:, :w], self.ones_b, sq[:, :w], k == 0, k == 7, [sq, self.cb], [psb], nowaw=k > 0)
        self.act(rstd[:, :w], psb[:, :w], AF.Sqrt, [psb], [rstd], scale=1.0 / D, bias=self.epsb[:, 0:1])
        self.kb.op("dve", lambda e: e.reciprocal(out=rstd[:, :w], in_=rstd[:, :w]), [rstd], [rstd])
        for k in range(8):
            if plain:
                self.stt("dve", hT[:, k, :w], xt[:, k, :w], gm_fn(k), rstd[:, :w], ALU.mult, ALU.mult,
                         [xt, rstd, self.small["fgain"]], [hT], nowaw=k > 0)
                continue
            tmp = tmpr.next()
            self.stt("dve", tmp[:, :w], xt[:, k, :w], gm_fn(k), rstd[:, :w], ALU.mult, ALU.mult,
                     [xt, rstd, self.gm], [tmp])
            self.act(hT[:, k, :w], tmp[:, :w], AF.Identity, [tmp, self.modv], [hT], nowaw=k > 0, bias=sh_fn(k), scale=1.0)

    def phase_p1(self, l, xsrc, last):
        kb = self.kb
        S = self.S
        kb.push()
        wext = kb.sb("wext", [128, 8, NC_EXT], BF16)
        wsrc = self.din["w_ext"]
        self.load_cast(wext, lambda c0, cw: wsrc[l, :, :, c0:c0 + cw], wsrc, NC_EXT, 8)
        import os
        P1S = int(os.environ.get("P1S", "99"))
        if P1S < 1:
            kb.pop(); return
        xr = kb.ring("xt", [128, 8, 512], F32, 2)
        hr = kb.ring("hT", [128, 8, 512], BF16, 2)
        sqr = kb.ring("sq", [128, 512], BF16, 2)
        tmpr = kb.ring("tmp", [128, 512], F32, 3)
        rstd_r = kb.ring("rstd", [128, 512], F32, 2)
        cosr = kb.ring("cos", [128, 512], F32, 2)
        sinr = kb.ring("sin", [128, 512], F32, 2)
        stg = kb.ring("stg", [128, 512], BF16, 6)
        r1 = kb.ring("r1", [128, 512], F32, 3)
        r2 = kb.ring("r2", [128, 512], F32, 3)
        sk = kb.ring("sk", [128, 256], BF16, 3)
        sv = kb.ring("sv", [128, 4, 65], BF16, 3)
        so = kb.ring("so", [128, 256], F32, 3)
        sa = kb.ring("sa", [128, 2, 65], BF16, 3)
        for b in sv.bufs + sa.bufs:
            kb.op("pool", lambda e: e.memset(b[:], 1.0), [], [b])
        psr = Ring(self.ps[0:6])
        psn = Ring(self.ps[6:8])
        xv = xsrc.t.rearrange("(k p) t -> p k t", p=128)

        def load(ti):
            e0, w, kind = self.tiles[ti]
            xt = xr.next()
            kb.dma(xt[:, :, :w], xv[:, :, e0:e0 + w], [xsrc], [xt], xt)
            cs = sn = None
            if kind == "x":
                cs, sn = cosr.next(), sinr.next()
                t0 = e0 - CTX
                kb.dma(cs[:, :w], self.din["cos_t"][:, t0:t0 + w], [self.din["cos_t"]], [cs], cs)
                kb.dma(sn[:, :w], self.din["sin_t"][:, t0:t0 + w], [self.din["sin_t"]], [sn], sn)
            return xt, cs, sn

        def fm(m, hT, w):
            ps = psr.next()
            for k in range(8):
                self.mm(ps[:, :w], wext[:, k, m * 128:(m + 1) * 128], hT[:, k, :w], k == 0, k == 7, [wext, hT], [ps], nowaw=k > 0)
            return ps

        def store_fm(dst, row0, e0, w, st):
            kb.dma(dst[row0:row0 + 128, e0:e0 + w], st[:, :w], [st], [dst], st, nowaw=True)

        pend = {0: load(0)}
        nt = len(self.tiles)
        for ti in range(nt):
            if ti + 1 < nt:
                pend[ti + 1] = load(ti + 1)
            e0, w, kind = self.tiles[ti]
            v = 0 if kind == "x" else 1
            xt, cs, sn = pend.pop(ti)
            hT = hr.next()
            self.norm_tile(xt, w, lambda k: self.gm[:, l, v, 0, k:k + 1], lambda k: self.modcol(l, v, 0, k),
                           hT, sqr, tmpr, rstd_r.next(), psn.next())
            if "hT" in self.dbg and l == 0:
                kb.dma(self.dbg_hT.t.rearrange("(k p) t -> p k t", p=128)[:, :, e0:e0 + w], hT[:, :, :w], [hT], [self.dbg_hT], hT, nowaw=True)
            if P1S < 2:
                continue
            for m in range(4):
                ps = fm(m, hT, w)
                st = stg.next()
                self.cp("act", st[:, :w], ps[:, :w], [ps], [st])
                store_fm(self.mqT if m < 2 else self.mkT, (m % 2) * 128, e0, w, st)
            if P1S < 3:
                continue
            need_ctx_q = not last
            for j in range(4):
                if kind == "c" and not need_ctx_q:
                    continue
                ps = fm(4 + j, hT, w)
                st = stg.next()
                if kind == "x":
                    ps2 = fm(8 + j, hT, w)
                    t1, t2 = r1.next(), r2.next()
                    self.tt("dve", t1[:, :w], ps[:, :w], cs[:, :w], ALU.mult, [ps, cs], [t1])
                    self.tt("dve", t2[:, :w], ps2[:, :w], sn[:, :w], ALU.mult, [ps2, sn], [t2])
                    self.tt("pool", st[:, :w], t1[:, :w], t2[:, :w], ALU.add, [t1, t2], [st])
                else:
                    self.cp("act", st[:, :w], ps[:, :w], [ps], [st])
                store_fm(self.aqT, j * 128, e0, w, st)
            for g in range(2):
                ps = fm(12 + g, hT, w)
                st = stg.next()
                if kind == "x":
                    ps2 = fm(14 + g, hT, w)
                    t1, t2 = r1.next(), r2.next()
                    self.tt("dve", t1[:, :w], ps[:, :w], cs[:, :w], ALU.mult, [ps, cs], [t1])
                    self.tt("dve", t2[:, :w], ps2[:, :w], sn[:, :w], ALU.mult, [ps2, sn], [t2])
                    self.tt("pool", st[:, :w], t1[:, :w], t2[:, :w], ALU.add, [t1, t2], [st])
                else:
                    self.cp("act", st[:, :w], ps[:, :w], [ps], [st])
                store_fm(self.akT, g * 128, e0, w, st)
            if P1S < 4:
                continue
            if not (kind == "c" and last):
                for ch in range(2):
                    psv_ = fm(16 + ch, hT, w)
                    psg_ = fm(18 + ch, hT, w)
                    sg = r1.next()
                    self.act(sg[:, :w], psg_[:, :w], AF.Sigmoid, [psg_], [sg])
                    st = stg.next()
                    self.tt("dve", st[:, :w], psv_[:, :w], sg[:, :w], ALU.mult, [psv_, sg], [st])
                    store_fm(self.yT, ch * 128, e0, w, st)
            if P1S < 5:
                continue
            for j in range(w // 128):
                blk = e0 // 128 + j
                psA, psB = psr.next(), psr.next()
                for k in range(8):
                    self.mm(psA[:, 0:512], hT[:, k, j * 128:(j + 1) * 128], wext[:, k, TM0:TM0 + 512], k == 0, k == 7,
                            [wext, hT], [psA], nowaw=k > 0)
                for k in range(8):
                    self.mm(psB[:, 0:400], hT[:, k, j * 128:(j + 1) * 128], wext[:, k, TM0 + 512:TM0 + 912], k == 0, k == 7,
                            [wext, hT], [psB], nowaw=k > 0)
                P1T = int(os.environ.get("P1T", "7"))
                if not (P1T & 2):
                    continue
                k_, v_, o_, a_ = sk.next(), sv.next(), so.next(), sa.next()
                self.cp("act", k_[:], psA[:, 0:256], [psA], [k_])
                self.cp("dve", v_[:, :, 0:64], psA[:, 256:512].rearrange("p (h d) -> p h d", h=4), [psA], [v_])
                self.cp("act", o_[:], psB[:, 0:256], [psB], [o_])
                self.cp("dve", self.gates[:, blk, :], psB[:, 256:272], [psB], [self.gates], nowaw=True)
                self.cp("dve", a_[:, :, 0:64], psB[:, 272:400].rearrange("p (h d) -> p h d", h=2), [psB], [a_])
                if not (P1T & 4):
                    continue
                kb.dma(self.mk_tm[:, blk, :], k_[:], [k_], [self.mk_tm], k_, nowaw=True)
                kb.dma(self.mv_tm[:, blk, :], v_[:].rearrange("p h d -> p (h d)"), [v_], [self.mv_tm], v_, nowaw=True)
                kb.dma(self.mo_tm[:, blk, :], o_[:], [o_], [self.mo_tm], o_, nowaw=True)
                kb.dma(self.av_tm[:, blk, :], a_[:].rearrange("p h d -> p (h d)"), [a_], [self.av_tm], a_, nowaw=True)
        kb.pop()

    def phase_mlstm(self, l, last):
        kb = self.kb
        NB = self.NB
        kb.push()
        gb = kb.sb("gb", [128, 16], F32)
        kb.dma(gb[:], self.din["gbias"][l:l + 1, :].partition_broadcast(128), [self.din["gbias"]], [gb], gb)
        hg = kb.sb("hg", [128, 256], F32)
        kb.dma(hg[:], self.din["hgain"][l:l + 1, :].partition_broadcast(128), [self.din["hgain"]], [hg], hg)
        G = self.gates
        gv = G[:].rearrange("p n (d g h) -> p n d g h", d=2, g=2)
        gbv = gb[:].rearrange("p (d g h) -> p d g h", d=2, g=2)
        lf = kb.sb("lf", [128, 2, NB, 4], F32)
        li = kb.sb("li", [128, 2, NB, 4], F32)
        Bc = kb.sb("Bc", [128, 2, NB, 4], F32)
        BT = kb.sb("BT", [128, 2, NB, 4], F32)
        A_ = kb.sb("A_", [128, 2, NB, 4], F32)
        E_ = kb.sb("E_", [128, 2, NB, 4], F32)
        G_ = kb.sb("G_", [128, 2, NB, 4], F32)
        DEC = kb.sb("DEC", [128, 2, NB, 2], F32)
        for d in range(2):
            bb = gbv[:, d, :, :].unsqueeze(1).to_broadcast([128, NB, 2, 4])
            self.tt("dve", li[:, d, :, :], gv[:, :, d, 0, :], gbv[:, d, 0, :].unsqueeze(1).to_broadcast([128, NB, 4]),
                    ALU.add, [G, gb], [li], nowaw=d > 0)
            self.tt("dve", lf[:, d, :, :], gv[:, :, d, 1, :], gbv[:, d, 1, :].unsqueeze(1).to_broadcast([128, NB, 4]),
                    ALU.add, [G, gb], [lf], nowaw=d > 0)
        fl = lambda b_: b_[:].rearrange("p a n h -> p (a n h)")
        self.act(fl(lf), fl(lf), AF.Exp, [lf], [lf], scale=-1.0)
        self.act(fl(lf), fl(lf), AF.Ln, [lf, self.cf], [lf], bias=self.cf[:, 3, 0:1], scale=1.0)
        kb.op("dve", lambda e: e.tensor_scalar_mul(out=fl(lf), in0=fl(lf), scalar1=-1.0), [lf], [lf])
        NC4 = NB * 4
        assert NC4 <= 512
        psb, pst = self.ps[0], self.ps[1]
        for d in range(2):
            tri = self.cf[:, 1 + d, :]
            rhs = lf[:, d, :, :].rearrange("p n h -> p (n h)")
            self.mm(psb[:, 0:NC4], tri, rhs, True, True, [self.cf, lf], [psb])
            self.mm(pst[:, 0:NC4], self.cf[:, 3, :], rhs, True, True, [self.cf, lf], [pst])
            self.cp("dve", Bc[:, d, :, :].rearrange("p n h -> p (n h)"), psb[:, 0:NC4], [psb], [Bc], nowaw=d > 0)
            self.cp("dve", BT[:, d, :, :].rearrange("p n h -> p (n h)"), pst[:, 0:NC4], [pst], [BT], nowaw=d > 0)
        self.tt("dve", fl(A_), fl(li), fl(Bc), ALU.subtract, [li, Bc], [A_])
        self.act(fl(A_), fl(A_), AF.Exp, [A_], [A_])
        kb.op("dve", lambda e: e.tensor_scalar_mul(out=fl(A_), in0=fl(A_), scalar1=0.125), [A_], [A_])
        self.act(fl(E_), fl(Bc), AF.Exp, [Bc], [E_])
        self.act(fl(BT), fl(BT), AF.Exp, [BT], [BT])
        self.tt("dve", fl(G_), fl(A_), fl(BT), ALU.mult, [A_, BT], [G_])
        for d in range(2):
            for pr in range(2):
                self.cp("dve", DEC[0:64, d, :, pr], BT[0:64, d, :, 2 * pr], [BT], [DEC], nowaw=(d + pr) > 0)
                self.cp("dve", DEC[64:128, d, :, pr], BT[64:128, d, :, 2 * pr + 1], [BT], [DEC], nowaw=True)
        import os
        M2S = int(os.environ.get("M2S", "99"))
        if M2S < 1:
            kb.pop(); return
        hsum = kb.sb("hsum", [128, NB, 256], F32)
        hsb = [Buf("hs%d" % i, hsum.t) for i in range(NB)]
        Cst = [kb.sb("Cst%d" % d, [128, 2, 130], F32) for d in range(2)]
        Cbf = [kb.sb("Cbf%d" % d, [128, 2, 130], BF16) for d in range(2)]
        for d in range(2):
            kb.op("pool", lambda e: e.memset(Cst[d][:], 0.0), [], [Cst[d]])
            kb.op("pool", lambda e: e.memset(Cbf[d][:], 0.0), [], [Cbf[d]])
        qr = kb.ring("qT", [128, 2, 128], BF16, 7)
        kr = kb.ring("kT", [128, 2, 128], BF16, 7)
        ktr = kb.ring("ktm", [128, 256], BF16, 7)
        vr = kb.ring("vtm", [128, 260], BF16, 7)
        orr = kb.ring("otm", [128, 256], F32, 7)
        pTr = kb.ring("pT", [128, 4, 128], BF16, 5)
        gkr = kb.ring("gk", [128, 256], BF16, 5)
        denr = kb.ring("den", [128, 8], F32, 4)
        sqh = kb.ring("sqh", [128, 256], F32, 2)
        ssr = kb.ring("ssr", [128, 8], F32, 2)
        mxr = kb.ring("mxr", [128, 256], F32, 2)
        sgr = kb.ring("sgr", [128, 256], F32, 2)
        mst = kb.ring("mst", [128, 2, 128], BF16, 3)
        psS_r = Ring([(self.ps[0], self.ps[1]), (self.ps[2], self.ps[3])])
        psU_r = Ring(self.ps[4:6])
        psD_r = Ring(self.ps[6:7])
        psT_r = Ring(self.ps[7:8])
        mqv = self.mqT.t.rearrange("(c p) t -> p c t", p=128)
        mkv = self.mkT.t.rearrange("(c p) t -> p c t", p=128)
        mixv = self.mixT.t[0:256, :].rearrange("(c p) t -> p c t", p=128)
        order = [list(range(NB)), [1, 0] + list(range(NB - 1, 1, -1))]
        step_of = [{blk: i for i, blk in enumerate(order[d])} for d in range(2)]

        def need_out(blk):
            return not (last and blk < 2)

        def load(d, blk):
            q, k, kt, v = qr.next(), kr.next(), ktr.next(), vr.next()
            c0 = blk * 128
            kb.dma(q[:], mqv[:, :, c0:c0 + 128], [self.mqT], [q], q)
            kb.dma(k[:], mkv[:, :, c0:c0 + 128], [self.mkT], [k], k)
            kb.dma(kt[:], self.mk_tm[:, blk, :], [self.mk_tm], [kt], kt)
            kb.dma(v[:], self.mv_tm[:, blk, :], [self.mv_tm], [v], v)
            o = None
            second = (step_of[1 - d][blk], 1 - d) < (step_of[d][blk], d)
            if second and need_out(blk):
                o = orr.next()
                kb.dma(o[:], self.mo_tm[:, blk, :], [self.mo_tm], [o], o)
            return q, k, kt, v, o, second

        stA = {}

        def compute_a(key, d, blk, q, k, kt, v, o, second):
            mask = self.cb[:, 1 + d, :]
            out_needed = need_out(blk)
            pT = None
            if out_needed:
                psS = psS_r.next()
                for h in range(4):
                    hp, pr = h % 2, h // 2
                    self.mm(psS[hp][:, pr * 128:(pr + 1) * 128], k[hp * 64:(hp + 1) * 64, pr, :], q[hp * 64:(hp + 1) * 64, pr, :],
                            True, True, [k, q], [psS[hp]], nowaw=pr > 0)
                pT = pTr.next()
                for h in range(4):
                    hp, pr = h % 2, h // 2
                    self.stt("dve", pT[:, h, :], psS[hp][:, pr * 128:(pr + 1) * 128], A_[:, d, blk, h:h + 1], mask, ALU.mult, ALU.mult,
                             [psS[hp], A_, self.cb], [pT], nowaw=h > 0)
            gk = gkr.next()
            self.tt("dve", gk[:].rearrange("p (h c) -> p h c", c=64), kt[:].rearrange("p (h c) -> p h c", c=64),
                    G_[:, d, blk, :].unsqueeze(2).to_broadcast([128, 4, 64]), ALU.mult, [kt, G_], [gk])
            stA[key] = (pT, gk)

        def compute(key, d, blk, q, k, kt, v, o, second):
            out_needed = need_out(blk)
            pT, gk = stA.pop(key)
            psU, psD = psU_r.next(), psD_r.next()
            if out_needed:
                for h in range(4):
                    hp, pr = h % 2, h // 2
                    self.mm(psU[:, h * 65:(h + 1) * 65], pT[:, h, :], v[:, h * 65:(h + 1) * 65], True, False, [pT, v], [psU], nowaw=h > 0)
                    self.mm(psU[:, h * 65:(h + 1) * 65], q[hp * 64:(hp + 1) * 64, pr, :],
                            Cbf[d][hp * 64:(hp + 1) * 64, pr, hp * 65:(hp + 1) * 65], False, True, [q, Cbf[d]], [psU], nowaw=True)
            for pr in range(2):
                self.mm(psD[:, pr * 130:(pr + 1) * 130], gk[:, pr * 128:(pr + 1) * 128], v[:, pr * 130:(pr + 1) * 130], True, True,
                        [gk, v], [psD], nowaw=pr > 0)
            for pr in range(2):
                self.stt("dve", Cst[d][:, pr, :], Cst[d][:, pr, :], DEC[:, d, blk, pr:pr + 1], psD[:, pr * 130:(pr + 1) * 130],
                         ALU.mult, ALU.add, [Cst[d], DEC, psD], [Cst[d]])
            self.cp("act", Cbf[d][:], Cst[d][:], [Cst[d]], [Cbf[d]])
            if not out_needed or M2S < 3:
                return
            den = denr.next()
            Uv = psU[:, 0:260].rearrange("p (h c) -> p h c", c=65)
            self.tt("dve", den[:, 0:4], Uv[:, :, 64], E_[:, d, blk, :], ALU.mult, [psU, E_], [den])
            self.stt("dve", den[:, 4:8], den[:, 0:4], -1.0, den[:, 0:4], ALU.mult, ALU.max, [den], [den])
            kb.op("dve", lambda e: e.tensor_scalar_max(out=den[:, 4:8], in0=den[:, 4:8], scalar1=1.0), [den], [den])
            kb.op("dve", lambda e: e.reciprocal(out=den[:, 4:8], in_=den[:, 4:8]), [den], [den])
            self.tt("dve", den[:, 0:4], E_[:, d, blk, :], den[:, 4:8], ALU.mult, [den, E_], [den])
            for h in range(4):
                hs = hsum[:, blk, h * 64:(h + 1) * 64]
                if not second:
                    self.act(hs, Uv[:, h, 0:64], AF.Identity, [psU, den], [hsb[blk]], nowaw=h > 0, scale=den[:, h:h + 1])
                else:
                    self.stt("dve", hs, Uv[:, h, 0:64], den[:, h:h + 1], hs, ALU.mult, ALU.add, [psU, den, hsb[blk]], [hsb[blk]])
            if not second or M2S < 4:
                return
            sq, ss = sqh.next(), ssr.next()
            hb = hsum[:, blk, :]
            self.tt("dve", sq[:], hb, hb, ALU.mult, [hsb[blk]], [sq])
            kb.op("dve", lambda e: e.reduce_sum(out=ss[:, 0:4], in_=sq[:].rearrange("p (h c) -> p h c", c=64), axis=AX.X), [sq], [ss])
            self.act(ss[:, 0:4], ss[:, 0:4], AF.Sqrt, [ss], [ss], scale=1.0 / 64.0, bias=self.epsb[:, 0:1])
            kb.op("dve", lambda e: e.reciprocal(out=ss[:, 4:8], in_=ss[:, 0:4]), [ss], [ss])
            mx, sg = mxr.next(), sgr.next()
            self.act(sg[:], o[:], AF.Sigmoid, [o], [sg])
            self.tt("dve", mx[:].rearrange("p (h c) -> p h c", c=64), hb.rearrange("p (h c) -> p h c", c=64),
                    ss[:, 4:8].unsqueeze(2).to_broadcast([128, 4, 64]), ALU.mult, [hsb[blk], ss], [mx])
            self.tt("pool", mx[:], mx[:], hg[:], ALU.mult, [mx, hg], [mx])
            self.tt("pool", mx[:], mx[:], sg[:], ALU.mult, [mx, sg], [mx])
            psT = psT_r.next()
            for c in range(2):
                kb.op("pe", lambda e: e.transpose(psT[:, c * 128:(c + 1) * 128], mx[:, c * 128:(c + 1) * 128], self.ident_f),
                      [mx, self.cf], [psT], nowaw=c > 0)
            st = mst.next()
            self.cp("act", st[:], psT[:, 0:256].rearrange("p (c t) -> p c t", c=2), [psT], [st])
            kb.dma(mixv[:, :, blk * 128:(blk + 1) * 128], st[:], [st], [self.mixT], st, nowaw=True)

        seq = []
        for i in range(NB):
            seq.append((0, order[0][i]))
            seq.append((1, order[1][i]))
        PF = 3
        LA = 2
        pend = {}
        for j in range(min(PF, len(seq))):
            pend[j] = load(*seq[j])
        for j in range(len(seq)):
            if j + PF < len(seq):
                pend[j + PF] = load(*seq[j + PF])
            compute_a(j, *seq[j], *pend[j])
            if j >= LA:
                compute(j - LA, *seq[j - LA], *pend.pop(j - LA))
        for j in range(max(len(seq) - LA, 0), len(seq)):
            compute(j, *seq[j], *pend.pop(j))
        kb.pop()

    def phase_attn(self, l, last):
        kb = self.kb
        NB, L = self.NB, self.L
        kb.push()
        aq = kb.sb("aq", [128, 4, L], BF16)
        ak = kb.sb("ak", [128, 2, L], BF16)
        av = kb.sb("av", [128, NB, 130], BF16)
        aqv = self.aqT.t.rearrange("(c p) t -> p c t", p=128)
        akv = self.akT.t.rearrange("(c p) t -> p c t", p=128)
        for c in range(2):
            kb.dma(ak[:, c, :], akv[:, c, :], [self.akT], [ak], ak, nowaw=c > 0)
        kb.dma(av[:], self.av_tm[:, :, :], [self.av_tm], [av], av)
        for c in range(4):
            kb.dma(aq[:, c, :], aqv[:, c, :], [self.aqT], [aq], aq, nowaw=c > 0)
        se = kb.sb("se", [128, 8], F32)
        srow = kb.sb("srow", [128, 8, 128], F32)
        kb.dma(se[64:65, :], self.din["sink"][l:l + 1, :], [self.din["sink"]], [se], se)
        self.act(se[64:65, :], se[64:65, :], AF.Exp, [se], [se])
        self.cp("dve", srow[64:65, :, :], se[64:65, :].unsqueeze(2).to_broadcast([1, 8, 128]), [se], [srow])
        pTr = kb.ring("apT", [128, 4, 128], BF16, 5)
        negm = kb.sb("negm", [128, 2, 2, 128], BF16)
        for i_ in range(2):
            for r_ in range(2):
                self.ts("dve", negm[:, i_, r_, :], self.cf[:, 1 + i_, :], -1.0, 30000.0, ALU.add, ALU.mult, [self.cf], [negm],
                        nowaw=(i_ + r_) > 0)
        rdr = kb.ring("rd", [128, 512], F32, 4)
        hir = kb.ring("hi", [128, 512], BF16, 3)
        lor = kb.ring("lo", [128, 512], BF16, 3)
        bcr = kb.ring("bc", [64, 512], F32, 2)
        ostr = kb.ring("ost", [64, 512], BF16, 3)
        psS_r = Ring([(self.ps[0], self.ps[1]), (self.ps[2], self.ps[3])])
        psO_r = Ring(self.ps[4:7])
        psB_r = Ring(self.ps[7:8])
        nxb = self.S // 128
        qblocks = ([] if last else [0, 1]) + list(range(2, NB))
        groups = []
        for qb in qblocks:
            if qb < 2:
                kbs = [(0, None), (1, None)]
            else:
                i = qb - 2
                kbs = [(0, None), (1, None)]
                if i >= 1:
                    kbs.append((qb - 1, 1))
                kbs.append((qb, None))
                if i < nxb - 1:
                    kbs.append((qb + 1, 0))
            for g in range(2):
                groups.append((qb, g, kbs))
        items = [(gi, n) for gi, grp in enumerate(groups) for n in range(len(grp[2]))]
        psO_of = {}
        pT_of = {}
        fin_state = {}

        def stage_a(it):
            gi, n = it
            qb, g, kbs = groups[gi]
            kbk, msk = kbs[n]
            psS = psS_r.next()
            for half in range(2):
                self.mm(psS[half][:, 0:256], ak[half * 64:(half + 1) * 64, g, kbk * 128:(kbk + 1) * 128],
                        aq[half * 64:(half + 1) * 64, 2 * g:2 * g + 2, qb * 128:(qb + 1) * 128], True, msk is None, [ak, aq], [psS[half]])
                if msk is not None:
                    self.mm(psS[half][:, 0:256], self.cb[:, 0, :], negm[:, msk, :, :], False, True,
                            [self.cb, negm], [psS[half]], nowaw=True)
            pT = pTr.next()
            pTv = pT[:].rearrange("p (jj h) q -> p jj h q", h=2)
            for half in range(2):
                self.act(pTv[:, :, half, :], psS[half][:, 0:256].rearrange("p (jj q) -> p jj q", jj=2), AF.Exp,
                         [psS[half]], [pT], nowaw=half > 0, scale=0.125)
            pT_of[it] = pT

        def stage_pv(it):
            gi, n = it
            qb, g, kbs = groups[gi]
            kbk, msk = kbs[n]
            if n == 0:
                psO_of[gi] = psO_r.next()
            psO = psO_of[gi]
            pT = pT_of.pop(it)
            pTf = pT[:].rearrange("p j q -> p (j q)")
            self.mm(psO[0:65, :], av[:, kbk, g * 65:(g + 1) * 65], pTf, n == 0, n == len(kbs) - 1, [av, pT], [psO], nowaw=n > 0)

        def fin_dve(gi):
            qb, g, kbs = groups[gi]
            psO = psO_of[gi]
            rd = rdr.next()
            self.tt("dve", rd[64:65, :], psO[64:65, :], srow[64:65, 4 * g:4 * g + 4, :].rearrange("p j q -> p (j q)"), ALU.add,
                    [psO, srow], [rd])
            kb.op("dve", lambda e: e.reciprocal(out=rd[64:65, :], in_=rd[64:65, :]), [rd], [rd])
            fin_state[gi] = rd

        def fin_pe(gi):
            qb, g, kbs = groups[gi]
            psO = psO_of.pop(gi)
            rd = fin_state.pop(gi)
            psB = psB_r.next()
            self.mm(psB[0:64, :], self.cf[64:65, 3, 0:64], rd[64:65, :], True, True, [self.cf, rd], [psB])
            bc = bcr.next()
            self.cp("act", bc[:], psB[0:64, :], [psB], [bc])
            ost = ostr.next()
            self.tt("dve", ost[:], psO[0:64, :], bc[:], ALU.mult, [psO, bc], [ost])
            r0 = 256 + 4 * g * 64
            dst = self.mixT.t[r0:r0 + 256, qb * 128:(qb + 1) * 128].rearrange("(j d) q -> d j q", d=64)
            kb.dma(dst, ost[:].rearrange("d (j q) -> d j q", j=4), [ost], [self.mixT], ost, nowaw=True)

        LA = 2
        pend = []
        finq = []

        def do_pv():
            it = pend.pop(0)
            stage_pv(it)
            gi, n = it
            ready = [g_ for (g_, age) in finq if age >= 1]
            finq[:] = [(g_, age + 1) for (g_, age) in finq if age < 1]
            if n == len(groups[gi][2]) - 1:
                fin_dve(gi)
                finq.append((gi, 0))
            for g_ in ready:
                fin_pe(g_)

        for it in items:
            stage_a(it)
            pend.append(it)
            if len(pend) > LA:
                do_pv()
        while pend:
            do_pv()
        for (g_, age) in finq:
            fin_pe(g_)
        kb.pop()

    def phase_conv(self, l, last):
        kb = self.kb
        kb.push()
        dww = kb.sb("dww", [128, 2, 31], F32)
        kb.dma(dww[:], self.din["dw_w"][l, :, :, :], [self.din["dw_w"]], [dww], dww)
        Dg = kb.sb("Dg", [128, 2, 31, 128], BF16)
        n = 0
        for ch in range(2):
            for k in range(31):
                eng = "pool" if n % 2 else "dve"
                kb.op(eng, lambda e: e.tensor_scalar_mul(out=Dg[:, ch, k, :], in0=self.cb[:, 0, :], scalar1=dww[:, ch, k:k + 1]),
                      [self.cb, dww], [Dg], nowaw=n > 0)
                n += 1
        pwf = kb.sb("pwf", [128, 2, 256], F32)
        pwb = kb.sb("pwb", [128, 2, 256], BF16)
        kb.dma(pwf[:], self.din["pw_w"][l, :, :, :], [self.din["pw_w"]], [pwf], pwf)
        self.cp("dve", pwb[:], pwf[:], [pwf], [pwb])
        dwb, lng, lnb = self.small["dw_b"], self.small["ln_g"], self.small["ln_b"]
        ytr = kb.ring("yt", [128, 2, 542], BF16, 3)
        yshr = kb.ring("ysh", [128, 2, 542], BF16, 3)
        zr = kb.ring("z", [128, 2, 512], F32, 3)
        zbr = kb.ring("zb", [128, 2, 512], BF16, 3)
        zqr = kb.ring("zq", [128, 2, 512], BF16, 3)
        mr = kb.ring("mean", [128, 512], F32, 2)
        vr = kb.ring("var", [128, 512], F32, 2)
        zcr = kb.ring("zc", [128, 512], F32, 2)
        sr = kb.ring("s", [128, 2, 512], BF16, 2)
        cst = kb.ring("cst", [128, 512], BF16, 3)
        psC_r = Ring(self.ps[0:4])
        psM_r = Ring(self.ps[4:5])
        psQ_r = Ring(self.ps[5:6])
        psP_r = Ring(self.ps[6:8])
        yv = self.yT.t.rearrange("(c p) t -> p c t", p=128)
        tiles = [t for t in self.tiles if not (last and t[2] == "c")]

        def load(ti):
            e0, w, kind = tiles[ti]
            lo_, hi_ = (0, CTX) if kind == "c" else (CTX, self.L)
            yt = ytr.next()
            a, b = max(e0 - 15, lo_), min(e0 + w + 15, hi_)
            if a > e0 - 15:
                kb.op("pool", lambda e: e.memset(yt[:, :, 0:15], 0.0), [], [yt])
            if b < e0 + w + 15:
                kb.op("pool", lambda e: e.memset(yt[:, :, w + 15:w + 30], 0.0), [], [yt], nowaw=a > e0 - 15)
            kb.dma(yt[:, :, a - (e0 - 15):b - (e0 - 15)], yv[:, :, a:b], [self.yT], [yt], yt, nowaw=(a > e0 - 15 or b < e0 + w + 15))
            ys = yshr.next()
            p0 = e0 - 15
            if a > p0:
                kb.op("pool", lambda e: e.memset(ys[:, :, 0:14], 0.0), [], [ys])
            if b < e0 + w + 15:
                kb.op("pool", lambda e: e.memset(ys[:, :, w + 14:w + 30], 0.0), [], [ys], nowaw=a > p0)
            a2 = max(a, p0 + 1)
            kb.dma(ys[:, :, a2 - p0 - 1:b - p0 - 1], yv[:, :, a2:b], [self.yT], [ys], ys, nowaw=(a > p0 or b < e0 + w + 15))
            return yt, ys

        def stage_a(ti, yts):
            e0, w, kind = tiles[ti]
            yt, ys = yts
            z, zb, zq = zr.next(), zbr.next(), zqr.next()
            for ch in range(2):
                psC = psC_r.next()
                for k in range(31):
                    src = yt[:, ch, k:k + w] if k % 2 == 0 else ys[:, ch, k - 1:k - 1 + w]
                    self.mm(psC[:, :w], Dg[:, ch, k, :], src, k == 0, k == 30, [Dg, yt, ys], [psC], nowaw=k > 0)
                self.act(z[:, ch, :w], psC[:, :w], AF.Identity, [psC, dwb], [z], nowaw=ch > 0, bias=dwb[:, l, ch:ch + 1], scale=1.0)
                self.act(zq[:, ch, :w], psC[:, :w], AF.Square, [psC, dwb], [zq], nowaw=ch > 0, bias=dwb[:, l, ch:ch + 1], scale=1.0)
                self.cp("dve", zb[:, ch, :w], z[:, ch, :w], [z], [zb], nowaw=ch > 0)
            return z, zb, zq

        def stage_b(ti, z, zb, zq):
            e0, w, kind = tiles[ti]
            psM, psQ = psM_r.next(), psQ_r.next()
            for ch in range(2):
                self.mm(psM[:, :w], self.ones256[:], zb[:, ch, :w], ch == 0, ch == 1, [self.ones256, zb], [psM], nowaw=ch > 0)
            for ch in range(2):
                self.mm(psQ[:, :w], self.ones256[:], zq[:, ch, :w], ch == 0, ch == 1, [self.ones256, zq], [psQ], nowaw=ch > 0)
            mean, var = mr.next(), vr.next()
            self.cp("act", mean[:, :w], psM[:, :w], [psM], [mean])
            self.act(var[:, :w], psM[:, :w], AF.Square, [psM], [var])
            self.tt("dve", var[:, :w], psQ[:, :w], var[:, :w], ALU.subtract, [psQ, var], [var])
            self.act(var[:, :w], var[:, :w], AF.Sqrt, [var], [var], bias=self.epsb[:, 0:1], scale=1.0)
            kb.op("dve", lambda e: e.reciprocal(out=var[:, :w], in_=var[:, :w]), [var], [var])
            s = sr.next()
            for ch in range(2):
                zc = zcr.next()
                self.tt("pool", zc[:, :w], z[:, ch, :w], mean[:, :w], ALU.subtract, [z, mean], [zc])
                self.tt("dve", zc[:, :w], zc[:, :w], var[:, :w], ALU.mult, [zc, var], [zc])
                self.act(s[:, ch, :w], zc[:, :w], AF.Silu, [zc, lng, lnb], [s], nowaw=ch > 0,
                         bias=lnb[:, l, ch:ch + 1], scale=lng[:, l, ch:ch + 1])
            for oc in range(2):
                psP = psP_r.next()
                for ic in range(2):
                    self.mm(psP[:, :w], pwb[:, ic, oc * 128:(oc + 1) * 128], s[:, ic, :w], ic == 0, ic == 1, [pwb, s], [psP], nowaw=ic > 0)
                st = cst.next()
                self.cp("act", st[:, :w], psP[:, :w], [psP], [st])
                kb.dma(self.mixT.t[768 + oc * 128:768 + (oc + 1) * 128, e0:e0 + w], st[:, :w], [st], [self.mixT], st, nowaw=True)

        pend = {0: load(0)}
        prevz = None
        for ti in range(len(tiles)):
            if ti + 1 < len(tiles):
                pend[ti + 1] = load(ti + 1)
            zs = stage_a(ti, pend.pop(ti))
            if prevz is not None:
                stage_b(ti - 1, *prevz)
            prevz = zs
        stage_b(len(tiles) - 1, *prevz)
        kb.pop()

    def phase_p3(self, l, xsrc, xdst, last):
        kb = self.kb
        kb.push()
        wo = kb.sb("wo", [128, 8, D], BF16)
        wsrc = self.din["w_out"]
        self.load_cast(wo, lambda c0, cw: wsrc[l, :, :, c0:c0 + cw], wsrc, D, 8)
        xr = kb.ring("xt", [128, 8, 512], F32, 2)
        mr = kb.ring("mt", [128, 8, 512], BF16, 2)
        orr = kb.ring("xo", [128, 8, 512], F32, 2)
        psr = Ring(self.ps)
        xv = xsrc.t.rearrange("(k p) t -> p k t", p=128)
        ov = xdst.t.rearrange("(k p) t -> p k t", p=128)
        mv = self.mixT.t.rearrange("(k p) t -> p k t", p=128)
        tiles = [t for t in self.tiles if not (last and t[2] == "c")]

        def load(ti):
            e0, w, kind = tiles[ti]
            xt, mt = xr.next(), mr.next()
            kb.dma(mt[:, :, :w], mv[:, :, e0:e0 + w], [self.mixT], [mt], mt)
            kb.dma(xt[:, :, :w], xv[:, :, e0:e0 + w], [xsrc], [xt], xt)
            return xt, mt

        pend = {0: load(0)}
        for ti in range(len(tiles)):
            if ti + 1 < len(tiles):
                pend[ti + 1] = load(ti + 1)
            e0, w, kind = tiles[ti]
            v = 0 if kind == "x" else 1
            xt, mt = pend.pop(ti)
            xo = orr.next()
            for m in range(8):
                ps = psr.next()
                for k in range(8):
                    self.mm(ps[:, :w], wo[:, k, m * 128:(m + 1) * 128], mt[:, k, :w], k == 0, k == 7, [wo, mt], [ps], nowaw=k > 0)
                self.stt("dve", xo[:, m, :w], ps[:, :w], self.modcol(l, v, 2, m), xt[:, m, :w], ALU.mult, ALU.add,
                         [ps, self.modv, xt], [xo], nowaw=m > 0)
            kb.dma(ov[:, :, e0:e0 + w], xo[:, :, :w], [xo], [xdst], xo, nowaw=True)
        kb.pop()

    def phase_f1(self, l, xsrc, last):
        kb = self.kb
        kb.push()
        wf = kb.sb("wfi", [128, 8, 2 * HID], BF16)
        wsrc = self.din["w_fi"]
        self.load_cast(wf, lambda c0, cw: wsrc[l, :, :, c0:c0 + cw], wsrc, 2 * HID, 8)
        xr = kb.ring("xt", [128, 8, 512], F32, 2)
        hr = kb.ring("hT", [128, 8, 512], BF16, 2)
        sqr = kb.ring("sq", [128, 512], BF16, 2)
        tmpr = kb.ring("tmp", [128, 512], F32, 3)
        rstd_r = kb.ring("rstd", [128, 512], F32, 2)
        sgr = kb.ring("sg", [128, 512], F32, 3)
        ast = kb.ring("ast", [128, 512], BF16, 6)
        psr = Ring(self.ps[0:6])
        psn = Ring(self.ps[6:8])
        xv = xsrc.t.rearrange("(k p) t -> p k t", p=128)
        tiles = [t for t in self.tiles if not (last and t[2] == "c")]

        def load(ti):
            e0, w, kind = tiles[ti]
            xt = xr.next()
            kb.dma(xt[:, :, :w], xv[:, :, e0:e0 + w], [xsrc], [xt], xt)
            return xt

        pend = {0: load(0)}
        for ti in range(len(tiles)):
            if ti + 1 < len(tiles):
                pend[ti + 1] = load(ti + 1)
            e0, w, kind = tiles[ti]
            v = 0 if kind == "x" else 1
            xt = pend.pop(ti)
            hT = hr.next()
            self.norm_tile(xt, w, lambda k: self.gm[:, l, v, 1, k:k + 1], lambda k: self.modcol(l, v, 3, k),
                           hT, sqr, tmpr, rstd_r.next(), psn.next())
            for j in range(HID // 128):
                psg, psu = psr.next(), psr.next()
                for k in range(8):
                    self.mm(psg[:, :w], wf[:, k, j * 128:(j + 1) * 128], hT[:, k, :w], k == 0, k == 7, [wf, hT], [psg], nowaw=k > 0)
                for k in range(8):
                    self.mm(psu[:, :w], wf[:, k, HID + j * 128:HID + (j + 1) * 128], hT[:, k, :w], k == 0, k == 7, [wf, hT], [psu], nowaw=k > 0)
                sg = sgr.next()
                self.act(sg[:, :w], psg[:, :w], AF.Silu, [psg], [sg])
                st = ast.next()
                self.tt("dve", st[:, :w], psu[:, :w], sg[:, :w], ALU.mult, [psu, sg], [st])
                kb.dma(self.aT.t[j * 128:(j + 1) * 128, e0:e0 + w], st[:, :w], [st], [self.aT], st, nowaw=True)
        kb.pop()

    def phase_f2(self, l, xsrc, xdst, last):
        kb = self.kb
        kb.push()
        NK = HID // 128
        wf = kb.sb("wfo", [128, NK, D], BF16)
        wsrc = self.din["w_fo"]
        self.load_cast(wf, lambda c0, cw: wsrc[l, :, :, c0:c0 + cw], wsrc, D, NK)
        xr = kb.ring("xt", [128, 8, 512], F32, 2)
        ar = kb.ring("at", [128, NK, 512], BF16, 2)
        orr = kb.ring("xo", [128, 8, 512], F32, 2)
        psr = Ring(self.ps)
        xv = xsrc.t.rearrange("(k p) t -> p k t", p=128)
        ov = xdst.t.rearrange("(k p) t -> p k t", p=128)
        av = self.aT.t.rearrange("(k p) t -> p k t", p=128)
        tiles = [t for t in self.tiles if not (last and t[2] == "c")]

        def load(ti):
            e0, w, kind = tiles[ti]
            xt, at = xr.next(), ar.next()
            kb.dma(at[:, 0:11, :w], av[:, 0:11, e0:e0 + w], [self.aT], [at], at)
            kb.dma(at[:, 11:NK, :w], av[:, 11:NK, e0:e0 + w], [self.aT], [at], at, nowaw=True)
            kb.dma(xt[:, :, :w], xv[:, :, e0:e0 + w], [xsrc], [xt], xt)
            return xt, at

        pend = {0: load(0)}
        for ti in range(len(tiles)):
            if ti + 1 < len(tiles):
                pend[ti + 1] = load(ti + 1)
            e0, w, kind = tiles[ti]
            v = 0 if kind == "x" else 1
            xt, at = pend.pop(ti)
            xo = orr.next()
            for m in range(8):
                ps = psr.next()
                for k in range(NK):
                    self.mm(ps[:, :w], wf[:, k, m * 128:(m + 1) * 128], at[:, k, :w], k == 0, k == NK - 1, [wf, at], [ps], nowaw=k > 0)
                self.stt("dve", xo[:, m, :w], ps[:, :w], self.modcol(l, v, 5, m), xt[:, m, :w], ALU.mult, ALU.add,
                         [ps, self.modv, xt], [xo], nowaw=m > 0)
            kb.dma(ov[:, :, e0:e0 + w], xo[:, :, :w], [xo], [xdst], xo, nowaw=True)
        kb.pop()

    def phase_final(self, xsrc):
        kb = self.kb
        kb.push()
        xr = kb.ring("xt", [128, 8, 512], F32, 2)
        yr = kb.ring("yT", [128, 8, 512], F32, 2)
        sqr = kb.ring("sq", [128, 512], BF16, 2)
        rstd_r = kb.ring("rstd", [128, 512], F32, 2)
        outr = kb.ring("yo", [128, D], F32, 3)
        psr = Ring(self.ps[0:6])
        psn = Ring(self.ps[6:8])
        xv = xsrc.t.rearrange("(k p) t -> p k t", p=128)
        fg = self.small["fgain"]
        tiles = [t for t in self.tiles if t[2] == "x"]

        def load(ti):
            e0, w, kind = tiles[ti]
            xt = xr.next()
            kb.dma(xt[:, :, :w], xv[:, :, e0:e0 + w], [xsrc], [xt], xt)
            return xt

        pend = {0: load(0)}
        for ti in range(len(tiles)):
            if ti + 1 < len(tiles):
                pend[ti + 1] = load(ti + 1)
            e0, w, kind = tiles[ti]
            xt = pend.pop(ti)
            yT = yr.next()
            self.norm_tile(xt, w, lambda k: fg[:, k:k + 1], None, yT, sqr, None, rstd_r.next(), psn.next(), plain=True)
            for j in range(w // 128):
                yo = outr.next()
                for half in range(2):
                    ps = psr.next()
                    for kk in range(4):
                        k = half * 4 + kk
                        kb.op("pe", lambda e: e.transpose(ps[:, kk * 128:(kk + 1) * 128], yT[:, k, j * 128:(j + 1) * 128], self.ident_f),
                              [yT, self.cf], [ps], nowaw=kk > 0)
                    self.cp("act" if half == 0 else "dve", yo[:, half * 512:(half + 1) * 512], ps[:, :], [ps], [yo], nowaw=half > 0)
                t0 = e0 - CTX + j * 128
                kb.dma(self.y[t0:t0 + 128, :], yo[:], [yo], [self.y], yo, nowaw=True)
        kb.pop()


_CACHE = {}


def kernel(**inputs):
    S = inputs["x"].shape[1]
    B = inputs["x"].shape[0]
    if S not in _CACHE:
        _CACHE[S] = Prog(S).build()
    nc = _CACHE[S]
    sh = prep_shared(inputs, S)
    in_maps = []
    for b in range(B):
        m = dict(sh)
        m.update(prep_core(inputs, b, S))
        in_maps.append(m)
    res = run_bass_kernel_spmd(nc, in_maps, core_ids=list(range(B)))
    return np.stack([np.asarray(r["y"], dtype=np.float32) for r in res.results], axis=0)
```
